# Optimizing a Trainium2 kernel written in Bass

```python
import math
import jax, jax.numpy as jnp
from jax import lax
import numpy as np

D_MODEL = 2048
BATCH = 1
SEQ = 8192
DEPTH = 1

CHUNK = 64
NORM_EPS = 1e-6
GDN_HEADS = 8
GDN_HEAD_DIM = 128
GDN_WIDTH = GDN_HEADS * GDN_HEAD_DIM
GDN_CONV = 4
ATT_HEADS = 8
ATT_HEAD_DIM = 128
ATT_WIDTH = ATT_HEADS * ATT_HEAD_DIM
ATT_LEFT_CHUNKS = 8
BAND = (ATT_LEFT_CHUNKS + 1) * CHUNK
REL_MAX = 256
REL_SIZE = REL_MAX + CHUNK
D_FF = 5504
IN_SPLIT = (GDN_WIDTH, GDN_WIDTH, GDN_WIDTH, GDN_WIDTH, GDN_HEADS, GDN_HEADS,
            ATT_WIDTH, ATT_WIDTH, ATT_WIDTH, D_MODEL, D_MODEL)
IN_WIDTH = sum(IN_SPLIT)

kernel_name = "hybrid_gdn_chunkattn_macaron_block"


def rms_norm(x, w):
    xf = x.astype(jnp.float32)
    y = xf * lax.rsqrt(jnp.mean(xf * xf, axis=-1, keepdims=True) + NORM_EPS)
    return (y * w.astype(jnp.float32)).astype(x.dtype)


def l2_norm(x):
    return x * lax.rsqrt(jnp.sum(x * x, axis=-1, keepdims=True) + NORM_EPS)


def swiglu(h, w_gate, w_up, w_down):
    return (jax.nn.silu(h @ w_gate) * (h @ w_up)) @ w_down


def causal_depthwise_conv_silu(x, w):
    K, C = w.shape
    y = lax.conv_general_dilated(
        x, w[:, None, :].astype(x.dtype), window_strides=(1,), padding=[(K - 1, 0)],
        dimension_numbers=("NWC", "WIO", "NWC"), feature_group_count=C)
    return jax.nn.silu(y)


def gated_delta_rule(q, k, v, g, beta):
    B, H, T, Dk = k.shape
    Dv = v.shape[-1]
    n = T // CHUNK
    q = q.reshape(B, H, n, CHUNK, Dk)
    k = k.reshape(B, H, n, CHUNK, Dk)
    v = v.reshape(B, H, n, CHUNK, Dv)
    g = g.reshape(B, H, n, CHUNK)
    beta = beta.reshape(B, H, n, CHUNK)

    G = jnp.cumsum(g, axis=-1)
    causal = jnp.tril(jnp.ones((CHUNK, CHUNK), dtype=bool))
    strict = jnp.tril(jnp.ones((CHUNK, CHUNK), dtype=bool), k=-1)
    decay = jnp.exp(jnp.where(causal, G[..., :, None] - G[..., None, :], -jnp.inf))

    kb = k * beta[..., None]
    A = jnp.where(strict, jnp.einsum('bhnid,bhnjd->bhnij', kb, k) * decay, 0.0)
    eye = jnp.eye(CHUNK, dtype=A.dtype)
    IA = A + eye
    u = lax.linalg.triangular_solve(IA, v * beta[..., None], left_side=True, lower=True)
    w = lax.linalg.triangular_solve(IA, kb * jnp.exp(G)[..., None], left_side=True, lower=True)

    Aqk = jnp.where(causal, jnp.einsum('bhnid,bhnjd->bhnij', q, k) * decay, 0.0)
    qg = q * jnp.exp(G)[..., None]
    kdec = k * jnp.exp(G[..., -1:] - G)[..., None]
    chunk_decay = jnp.exp(G[..., -1])

    def step(S, inp):
        qg_c, kdec_c, u_c, w_c, aqk_c, cd_c = inp
        v_new = u_c - jnp.einsum('bhid,bhde->bhie', w_c, S)
        o_c = jnp.einsum('bhid,bhde->bhie', qg_c, S) + jnp.einsum('bhij,bhje->bhie', aqk_c, v_new)
        S = S * cd_c[..., None, None] + jnp.einsum('bhid,bhie->bhde', kdec_c, v_new)
        return S, o_c

    xs = tuple(jnp.moveaxis(a, 2, 0) for a in (qg, kdec, u, w, Aqk, chunk_decay))
    S0 = jnp.zeros((B, H, Dk, Dv), dtype=q.dtype)
    _, o = lax.scan(step, S0, xs)
    return jnp.moveaxis(o, 0, 2).reshape(B, H, T, Dv)


def gdn_branch(q, k, v, z, a, b, conv_w, A_log, dt_bias, out_norm_w):
    B, T, _ = q.shape
    qkv = causal_depthwise_conv_silu(jnp.concatenate([q, k, v], axis=-1), conv_w)
    q, k, v = jnp.split(qkv, 3, axis=-1)
    heads = lambda t: t.reshape(B, T, GDN_HEADS, GDN_HEAD_DIM).transpose(0, 2, 1, 3).astype(jnp.float32)
    qh = l2_norm(heads(q)) * (GDN_HEAD_DIM ** -0.5)
    kh = l2_norm(heads(k))
    vh = heads(v)
    g = -jnp.exp(A_log.astype(jnp.float32)) * jax.nn.softplus(a.astype(jnp.float32) + dt_bias.astype(jnp.float32))
    beta = jax.nn.sigmoid(b.astype(jnp.float32))
    o = gated_delta_rule(qh, kh, vh, g.transpose(0, 2, 1), beta.transpose(0, 2, 1))
    o = o.transpose(0, 2, 1, 3)
    zf = z.reshape(B, T, GDN_HEADS, GDN_HEAD_DIM).astype(jnp.float32)
    o = rms_norm(o, out_norm_w) * jax.nn.silu(zf)
    return o.reshape(B, T, GDN_WIDTH).astype(z.dtype)


def chunk_attention_branch(q, k, v, q_norm_w, k_norm_w, rel_bias):
    B, T, _ = q.shape
    n = T // CHUNK
    pad = ATT_LEFT_CHUNKS * CHUNK
    heads = lambda t: t.reshape(B, T, ATT_HEADS, ATT_HEAD_DIM).transpose(0, 2, 1, 3)
    qh = rms_norm(heads(q), q_norm_w).astype(jnp.float32)
    kh = rms_norm(heads(k), k_norm_w).astype(jnp.float32)
    vh = heads(v).astype(jnp.float32)

    def band(t):
        tp = jnp.pad(t, ((0, 0), (0, 0), (pad, 0), (0, 0))).reshape(B, ATT_HEADS, n + ATT_LEFT_CHUNKS, CHUNK, ATT_HEAD_DIM)
        return jnp.stack([tp[:, :, s:s + n] for s in range(ATT_LEFT_CHUNKS + 1)], axis=3).reshape(
            B, ATT_HEADS, n, BAND, ATT_HEAD_DIM)

    k_band = band(kh)
    v_band = band(vh)
    qc = qh.reshape(B, ATT_HEADS, n, CHUNK, ATT_HEAD_DIM)
    s = jnp.einsum('bhnqd,bhnkd->bhnqk', qc, k_band) * (ATT_HEAD_DIM ** -0.5)

    i = jnp.arange(CHUNK)[:, None]
    j = jnp.arange(BAND)[None, :]
    rel_idx = jnp.clip(i - j + pad, -(CHUNK - 1), REL_MAX) + (CHUNK - 1)
    bias = rel_bias.astype(jnp.float32)[:, rel_idx]
    key_pos = (jnp.arange(n)[:, None] - ATT_LEFT_CHUNKS) * CHUNK + jnp.arange(BAND)[None, :]
    valid = key_pos >= 0
    s = jnp.where(valid[:, None, :], s + bias[:, None], -jnp.inf)
    p = jax.nn.softmax(s, axis=-1)
    o = jnp.einsum('bhnqk,bhnkd->bhnqd', p, v_band).reshape(B, ATT_HEADS, T, ATT_HEAD_DIM)
    return o.transpose(0, 2, 1, 3).reshape(B, T, ATT_WIDTH).astype(q.dtype)


def setup_inputs(seed: int = 0) -> dict:
    key = jax.random.key(seed)
    ks = jax.random.split(key, 24)
    f32 = jnp.float32
    L = DEPTH
    nrm = lambda k, shape, fan_in: jax.random.normal(k, shape, f32) * (fan_in ** -0.5)
    gain = lambda k, shape: 1.0 + 0.02 * jax.random.normal(k, shape, f32)
    dt = jnp.exp(jax.random.uniform(ks[8], (L, GDN_HEADS), f32, math.log(1e-3), math.log(1e-1)))
    return {
        "x": jax.random.normal(ks[0], (BATCH, SEQ, D_MODEL), f32),
        "ffn1_norm": gain(ks[1], (L, D_MODEL)),
        "ffn1_w_gate": nrm(ks[2], (L, D_MODEL, D_FF), D_MODEL),
        "ffn1_w_up": nrm(ks[3], (L, D_MODEL, D_FF), D_MODEL),
        "ffn1_w_down": nrm(ks[4], (L, D_FF, D_MODEL), D_FF),
        "mix_norm": gain(ks[5], (L, D_MODEL)),
        "w_in": nrm(ks[6], (L, D_MODEL, IN_WIDTH), D_MODEL),
        "gdn_conv": nrm(ks[7], (L, GDN_CONV, 3 * GDN_WIDTH), GDN_CONV),
        "gdn_A_log": jnp.log(jax.random.uniform(ks[9], (L, GDN_HEADS), f32, 1.0, 16.0)),
        "gdn_dt_bias": dt + jnp.log(-jnp.expm1(-dt)),
        "gdn_out_norm": gain(ks[10], (L, GDN_HEAD_DIM)),
        "att_q_norm": gain(ks[11], (L, ATT_HEAD_DIM)),
        "att_k_norm": gain(ks[12], (L, ATT_HEAD_DIM)),
        "att_rel_bias": 0.1 * jax.random.normal(ks[13], (L, ATT_HEADS, REL_SIZE), f32),
        "w_branch_gdn": nrm(ks[14], (L, GDN_WIDTH, D_MODEL), GDN_WIDTH),
        "w_branch_att": nrm(ks[15], (L, ATT_WIDTH, D_MODEL), ATT_WIDTH),
        "w_out": nrm(ks[16], (L, D_MODEL, D_MODEL), D_MODEL),
        "ffn2_norm": gain(ks[17], (L, D_MODEL)),
        "ffn2_w_gate": nrm(ks[18], (L, D_MODEL, D_FF), D_MODEL),
        "ffn2_w_up": nrm(ks[19], (L, D_MODEL, D_FF), D_MODEL),
        "ffn2_w_down": nrm(ks[20], (L, D_FF, D_MODEL), D_FF),
    }


def reference(x, ffn1_norm, ffn1_w_gate, ffn1_w_up, ffn1_w_down, mix_norm, w_in, gdn_conv,
              gdn_A_log, gdn_dt_bias, gdn_out_norm, att_q_norm, att_k_norm, att_rel_bias,
              w_branch_gdn, w_branch_att, w_out, ffn2_norm, ffn2_w_gate, ffn2_w_up, ffn2_w_down):
    split_at = np.cumsum(IN_SPLIT)[:-1].tolist()
    for l in range(DEPTH):
        h = rms_norm(x, ffn1_norm[l])
        x = x + 0.5 * swiglu(h, ffn1_w_gate[l], ffn1_w_up[l], ffn1_w_down[l])

        h = rms_norm(x, mix_norm[l])
        proj = h @ w_in[l]
        (gq, gk, gv, gz, ga, gb, aq, ak, av, gate_gdn, gate_att) = jnp.split(proj, split_at, axis=-1)
        o_gdn = gdn_branch(gq, gk, gv, gz, ga, gb, gdn_conv[l], gdn_A_log[l], gdn_dt_bias[l], gdn_out_norm[l])
        o_att = chunk_attention_branch(aq, ak, av, att_q_norm[l], att_k_norm[l], att_rel_bias[l])
        merged = (jax.nn.sigmoid(gate_gdn) * (o_gdn @ w_branch_gdn[l])
                  + jax.nn.sigmoid(gate_att) * (o_att @ w_branch_att[l]))
        x = x + merged @ w_out[l]

        h = rms_norm(x, ffn2_norm[l])
        x = x + 0.5 * swiglu(h, ffn2_w_gate[l], ffn2_w_up[l], ffn2_w_down[l])
    return x
```

```python
import contextlib
import os
import numpy as np
import concourse.bass as bass
import concourse.mybir as mybir
from concourse.bass_utils import run_bass_kernel_spmd

F32 = mybir.dt.float32
BF16 = mybir.dt.bfloat16
AF = mybir.ActivationFunctionType
ALU = mybir.AluOpType

NCORE = 8
D = 2048
SEQ = 8192
TC = 1024
NF = 43
EPS = 1e-6
SB_BASE = 16512
SB_END = 229376
NEG = -30000.0


class Tok:
    __slots__ = ("w", "r")

    def __init__(self):
        self.w = None
        self.r = {}


class V:
    __slots__ = ("ap", "toks")

    def __init__(self, ap, toks):
        self.ap = ap
        self.toks = toks


class T:
    def __init__(self, handle):
        self.h = handle
        self.toks = {}

    def k(self, key, ap):
        if key not in self.toks:
            self.toks[key] = Tok()
        return V(ap, [self.toks[key]])

    def all(self, ap):
        return V(ap, list(self.toks.values()))

    def __getitem__(self, idx):
        return self.k(None, self.h[idx])

    def v(self, ap):
        return self.k(None, ap)


class Sched:
    ENGS = ("pe", "act", "dve", "pool", "sp")

    def __init__(self, nc, stack, n_dma_sems=24):
        self.nc = nc
        self.stack = stack
        self.ops = {e: [] for e in self.ENGS}
        self.cnt = {e: 0 for e in self.ENGS}
        self.waited = {e: {} for e in self.ENGS}
        self.sems = {}
        for e in self.ENGS:
            self.sems[e] = stack.enter_context(nc.semaphore("s_" + e))
        self.dma_sems = {"sp": [], "pool": []}
        self.dma_val = {}
        for q, n in (("sp", 16), ("pool", 8)):
            for i in range(n):
                k = "d%s%d" % (q, i)
                self.sems[k] = stack.enter_context(nc.semaphore("s_" + k))
                self.dma_sems[q].append(k)
                self.dma_val[k] = 0
        self.dma_rr = {"sp": 0, "pool": 0}
        self.n_cc = 0

    def _collect(self, eng, reads, writes):
        waits = {}

        def need(ev):
            if ev is None:
                return
            k, v = ev
            if eng == "pe" and k == "pe":
                return
            if self.waited[eng].get(k, 0) >= v:
                return
            if waits.get(k, 0) < v:
                waits[k] = v

        for t in reads:
            need(t.w)
        for t in writes:
            need(t.w)
            for k, v in t.r.items():
                need((k, v))
        for k, v in waits.items():
            self.waited[eng][k] = v
        return waits

    def _mark(self, ev, reads, writes):
        k, v = ev
        for t in reads:
            if t.r.get(k, 0) < v:
                t.r[k] = v
        for t in writes:
            t.w = ev
            t.r = {}

    def op(self, eng, fn, reads=(), writes=(), inc=True):
        waits = self._collect(eng, reads, writes)
        if inc:
            self.cnt[eng] += 1
            ev = (eng, self.cnt[eng])
        else:
            assert eng == "pe"
            ev = (eng, self.cnt[eng] + 1)
        self._mark(ev, reads, writes)
        self.ops[eng].append((list(waits.items()), fn, (eng, 1) if inc else None))

    def dma(self, q, out, in_, reads=(), writes=(), in_fn=None, own_sem=False):
        waits = self._collect(q, reads, writes)
        if own_sem:
            k = "x%d" % self.n_cc
            self.n_cc += 1
            self.sems[k] = self.stack.enter_context(self.nc.semaphore("s_" + k))
            self.dma_val[k] = 0
        else:
            k = self.dma_sems[q][self.dma_rr[q]]
            self.dma_rr[q] = (self.dma_rr[q] + 1) % len(self.dma_sems[q])
        if self.dma_val[k] > 0 and self.waited[q].get(k, 0) < self.dma_val[k]:
            waits[k] = self.dma_val[k]
            self.waited[q][k] = self.dma_val[k]
        self.dma_val[k] += 16
        ev = (k, self.dma_val[k])
        self._mark(ev, reads, writes)
        if in_fn is None:
            fn = lambda e: e.dma_start(out=out, in_=in_)
        else:
            fn = lambda e: e.dma_start(out=out, in_=in_fn(e))
        self.ops[q].append((list(waits.items()), fn, (k, 16)))

    def collective(self, kind, alu, ins, outs, reads=(), writes=()):
        q = "pool"
        waits = self._collect(q, reads, writes)
        k = "cc%d" % self.n_cc
        self.n_cc += 1
        self.sems[k] = self.stack.enter_context(self.nc.semaphore("s_" + k))
        ev = (k, 1)
        self._mark(ev, reads, writes)

        def fn(e):
            return e.collective_compute(kind, alu, replica_groups=[list(range(NCORE))],
                                        ins=ins, outs=outs)
        self.ops[q].append((list(waits.items()), fn, (k, 1)))

    def fence(self, engs=("pe", "act", "dve", "sp")):
        snap = {e: self.cnt[e] for e in engs}
        dsnap = dict(self.dma_val)
        for e in engs:
            waits = {}
            for o in engs:
                if o != e and self.waited[e].get(o, 0) < snap[o]:
                    waits[o] = snap[o]
            for k, v in dsnap.items():
                if v > 0 and self.waited[e].get(k, 0) < v:
                    waits[k] = v
            for k, v in waits.items():
                self.waited[e][k] = v
            self.ops[e].append((list(waits.items()), None, None))

    def finish(self, toks):
        waits = self._collect("sp", [], toks)
        self.ops["sp"].append((list(waits.items()), None, None))

    def check(self):
        sem = {}
        pc = {e: 0 for e in self.ENGS}
        progress = True
        while progress:
            progress = False
            for e in self.ENGS:
                while pc[e] < len(self.ops[e]):
                    waits, fn, inc = self.ops[e][pc[e]]
                    if all(sem.get(k, 0) >= v for k, v in waits):
                        if fn is not None and inc is not None:
                            sem[inc[0]] = sem.get(inc[0], 0) + inc[1]
                        pc[e] += 1
                        progress = True
                    else:
                        break
        stuck = {e: (pc[e], len(self.ops[e]), self.ops[e][pc[e]][0]) for e in self.ENGS
                 if pc[e] < len(self.ops[e])}
        assert not stuck, stuck

    def emit(self, block):
        self.check()
        sems = self.sems

        def run(name):
            def f(e):
                if name == "sp":
                    self.pid = e.partition_id()
                    self.off = [e.snap(self.pid * TC + h * 512, min_val=0, max_val=SEQ - 512) for h in range(2)]
                for waits, fn, inc in self.ops[name]:
                    for k, v in waits:
                        e.wait_ge(sems[k], v)
                    if fn is None:
                        continue
                    try:
                        ins = fn(e)
                    except Exception:
                        print("EMIT FAIL", name, self.ops[name].index((waits, fn, inc)), len(self.ops[name]), waits, inc)
                        import traceback; traceback.print_exc()
                        try:
                            fn(e); print("RETRY OK")
                        except Exception as ex2:
                            print("RETRY FAIL", ex2)
                        raise
                    if inc is not None:
                        ins.then_inc(sems[inc[0]], inc[1])
            return f

        block.tensor(run("pe"))
        block.scalar(run("act"))
        block.vector(run("dve"))
        block.gpsimd(run("pool"))
        block.sync(run("sp"))


class Region:
    def __init__(self, nc, name, base, size):
        self.nc, self.name, self.base, self.size, self.off, self.n = nc, name, base, size, 0, 0

    def reset(self):
        self.off = 0

    def alloc(self, shape, dtype):
        nb = int(np.prod(shape[1:])) * (4 if dtype == F32 else 2)
        nb = (nb + 31) // 32 * 32
        assert self.off + nb <= self.size, (self.name, self.off, nb, self.size)
        h = self.nc.alloc_sbuf_tensor_at("%s_%d" % (self.name, self.n), list(shape), dtype,
                                         offset=self.base + self.off)
        self.off += nb
        self.n += 1
        return T(h)


def build(stop_after=None):
    nc = bass.Bass("TRN2", target_bir_lowering=False)
    dram_in = lambda n, s, dt=F32: nc.dram_tensor(n, s, dt, kind="ExternalInput").ap()
    xT_d = dram_in("xT", [128, 16 * TC])
    nrm_d = dram_in("nrm", [128, 48])
    wgu_d = [dram_in("wgu%d" % i, [NF, 128, 4096]) for i in (1, 2)]
    wd_d = [dram_in("wd%d" % i, [2, 16, 128, 22 * 128]) for i in (1, 2)]
    wh_d = dram_in("wh", [128, 16 * 898])
    wgt_d = dram_in("wgt", [16, 128, 4096])
    wbr_d = dram_in("wbr", [16, 128, 2048])
    wo_d = dram_in("wo", [16, 128, 2048])
    cw_d = dram_in("cw", [128, 12])
    hvec_d = dram_in("hvec", [128, 8])
    bt_d = dram_in("bt", [128, 5 * 128])
    cst_d = dram_in("cst", [128, 128 + 128 + 64 + 512 + 512 + 512])
    yT_d = nc.dram_tensor("yT", [128, 16 * TC], F32, kind="ExternalOutput").ap()
    ag1_in = nc.dram_tensor("ag1_in", [D, TC], BF16)
    ag1_out = nc.dram_tensor("ag1_out", [NCORE * D, TC], BF16)
    ag2_in = nc.dram_tensor("ag2_in", [256, SEQ], BF16)
    ag2_out = nc.dram_tensor("ag2_out", [NCORE * 256, SEQ], BF16)
    xsp = nc.dram_tensor("xsp", [128, 16 * TC], F32)
    tk_ag1_in, tk_ag1_out, tk_ag2_in, tk_ag2_out, tk_xsp, tk_y = (Tok() for _ in range(6))

    with contextlib.ExitStack() as st:
        S = Sched(nc, st)
        R0 = Region(nc, "r0", SB_BASE, 65536)
        R1 = Region(nc, "r1", SB_BASE + 65536, 32768)
        R2 = Region(nc, "r2", SB_BASE + 98304, 45056)
        R3 = Region(nc, "r3", SB_BASE + 143360, 32768)
        R4 = Region(nc, "r4", SB_BASE + 176128, SB_END - SB_BASE - 176128)

        banks = [T(st.enter_context(nc.psum_tensor("ps%d" % i, [128, 512], F32))) for i in range(8)]

        class Rot:
            def __init__(self, items):
                self.items, self.i = items, 0

            def next(self):
                x = self.items[self.i % len(self.items)]
                self.i += 1
                return x

        def OP(eng, method, out, *args, **kw):
            vs = [a for a in list(args) + list(kw.values()) if isinstance(a, V)]

            def fn(e):
                a2 = [a.ap if isinstance(a, V) else a for a in args]
                k2 = {k: (v.ap if isinstance(v, V) else v) for k, v in kw.items()}
                return getattr(e, method)(out.ap, *a2, **k2)
            reads = [t for v in vs for t in v.toks]
            S.op(eng, fn, reads=reads, writes=out.toks)

        def MM(out, lhsT, rhs, start=True, stop=True, inc=None):
            def fn(e):
                return e.matmul(out.ap, lhsT=lhsT.ap, rhs=rhs.ap, start=start, stop=stop)
            S.op("pe", fn, reads=lhsT.toks + rhs.toks, writes=out.toks, inc=(stop if inc is None else inc))

        def TR(out, in_, ident):
            def fn(e):
                return e.transpose(out.ap, in_.ap, ident.ap)
            S.op("pe", fn, reads=in_.toks + ident.toks, writes=out.toks, inc=True)

        def DMA(q, out, in_, in_fn=None):
            S.dma(q, out.ap, in_.ap, reads=in_.toks, writes=out.toks, in_fn=in_fn)

        xT = T(nc.alloc_sbuf_tensor_at("xT_sb", [128, 16, TC], F32, offset=R0.base))
        hT = T(nc.alloc_sbuf_tensor_at("hT_sb", [128, 16, TC], BF16, offset=R1.base))
        actT = T(nc.alloc_sbuf_tensor_at("actT_sb", [128, 22, TC], BF16, offset=R2.base))
        ring = Rot([T(nc.alloc_sbuf_tensor_at("ring%d" % i, [128, 4096], BF16,
                                              offset=R3.base + i * 8192)) for i in range(4)])
        nrm = R4.alloc([128, 48], F32)
        cw = R4.alloc([128, 12], F32)
        hvec = R4.alloc([128, 8], F32)
        bt = R4.alloc([128, 5, 128], F32)
        cst = R4.alloc([128, 1856], F32)
        ones_bf = R4.alloc([128, 128], BF16)
        ident_bf = R4.alloc([128, 128], BF16)
        rstd = R4.alloc([128, TC], F32)
        tmpA = [R4.alloc([128, 512], F32) for _ in range(2)]
        sqb = [R4.alloc([128, 512], BF16) for _ in range(2)]
        negA = R4.alloc([128, 1], F32)
        epsc = R4.alloc([128, 2], F32)
        ones_f = cst.v(cst.h[:, 0:128])
        ident_f = cst.v(cst.h[:, 128:256])
        tri_f = cst.v(cst.h[0:64, 256:320])
        m_incl = cst.v(cst.h[0:64, 320:832])
        m_strict = cst.v(cst.h[0:64, 832:1344])
        eye8 = cst.v(cst.h[0:64, 1344:1856])

        def xk(kc, h):
            return xT.k((kc, h), xT.h[:, kc, h * 512:(h + 1) * 512])

        def hk(kc, h):
            return hT.k((kc, h), hT.h[:, kc, h * 512:(h + 1) * 512])

        def ak(f, h):
            return actT.k((f, h), actT.h[:, f, h * 512:(h + 1) * 512])

        for i in range(4):
            for h in range(2):
                pass
        for kc in range(16):
            for h in range(2):
                DMA("sp", xk(kc, h),
                    V(xT_d.rearrange("p (k t) -> p k t", k=16)[:, kc, h * 512:(h + 1) * 512], []))
        DMA("sp", nrm[:, :], V(nrm_d, []))
        DMA("sp", cw[:, :], V(cw_d, []))
        DMA("sp", hvec[:, :], V(hvec_d, []))
        DMA("sp", bt.v(bt.h[:, :, :]), V(bt_d.rearrange("p (r i) -> p r i", r=5), []))
        DMA("sp", cst[:, :], V(cst_d, []))
        OP("dve", "tensor_copy", ones_bf[:, :], ones_f)
        OP("dve", "tensor_copy", ident_bf[:, :], ident_f)
        OP("dve", "memset", bt.v(bt.h[0:64, 0, 64:128]), NEG)
        OP("dve", "memset", bt.v(bt.h[64:128, 4, 0:64]), NEG)
        OP("dve", "memset", epsc.v(epsc.h[:, 0:1]), EPS)
        OP("dve", "memset", epsc.v(epsc.h[:, 1:2]), 1.0)
        OP("act", "activation", negA[:, :], hvec.v(hvec.h[:, 3:4]), AF.Exp)
        OP("dve", "tensor_scalar", negA[:, :], negA[:, :], -1.0, None, ALU.mult)

        allb = Rot(banks)

        def rmsnorm(n_idx):
            for h in range(2):
                ps = allb.next()
                for kc in range(16):
                    sq = sqb[kc % 2]
                    OP("act", "activation", sq[:, :], xk(kc, h), AF.Square)
                    MM(ps[:, :], ones_bf[:, :], sq[:, :], start=(kc == 0), stop=(kc == 15), inc=True)
                rh = rstd.k(h, rstd.h[:, h * 512:(h + 1) * 512])
                OP("act", "activation", rh, ps[:, :], AF.Ln, scale=1.0 / D, bias=epsc.v(epsc.h[:, 0:1]))
                OP("act", "activation", rh, rh, AF.Exp, scale=-0.5)
                for kc in range(16):
                    OP("dve", "scalar_tensor_tensor", hk(kc, h), xk(kc, h),
                       nrm.v(nrm.h[:, n_idx * 16 + kc:n_idx * 16 + kc + 1]), rh, ALU.mult, ALU.mult)

        def ffn(wgu, wd, n_idx):
            rmsnorm(n_idx)
            for fh in range(2):
                nf = 22 if fh == 0 else 21
                if os.environ.get("MK_FAST"):
                    nf = int(os.environ["MK_FAST"])
                for fl in range(nf):
                    f = fh * 22 + fl
                    slot = ring.next()
                    DMA("pool", slot[:, :], V(wgu[f], []))
                    w = slot.h[:, :].rearrange("p (k g c) -> p k g c", k=16, g=2)
                    for h in range(2):
                        pg, pu = allb.next(), allb.next()
                        for kc in range(16):
                            MM(pg[:, :], slot.v(w[:, kc, 0, :]), hk(kc, h), start=(kc == 0), stop=(kc == 15))
                        for kc in range(16):
                            MM(pu[:, :], slot.v(w[:, kc, 1, :]), hk(kc, h), start=(kc == 0), stop=(kc == 15))
                        tmp = tmpA[(fl * 2 + h) % 2]
                        OP("act", "activation", tmp[:, :], pg[:, :], AF.Silu)
                        OP("dve", "tensor_tensor", ak(fl, h), tmp[:, :], pu[:, :], ALU.mult)
                for dc in range(16 if not os.environ.get("MK_FAST") else 2):
                    slot = ring.next()
                    DMA("pool", slot.v(slot.h[:, 0:nf * 128]), V(wd[fh, dc][:, 0:nf * 128], []))
                    w = slot.h[:, 0:22 * 128].rearrange("p (f c) -> p f c", f=22)
                    for h in range(2):
                        pd = allb.next()
                        for fl in range(nf):
                            MM(pd[:, :], slot.v(w[:, fl, :]), ak(fl, h), start=(fl == 0), stop=(fl == nf - 1))
                        OP("dve", "scalar_tensor_tensor", xk(dc, h), pd[:, :], 0.5, xk(dc, h),
                           ALU.mult, ALU.add)

        def write_out():
            for kc in range(16):
                for h in range(2):
                    S.dma("sp", yT_d.rearrange("p (k t) -> p k t", k=16)[:, kc, h * 512:(h + 1) * 512],
                          xk(kc, h).ap, reads=xk(kc, h).toks, writes=[tk_y])
            S.finish([tk_y])

        if not os.environ.get("MK_NOFFN"):
            ffn(wgu_d[0], wd_d[0], 0)
        if stop_after == "A":
            write_out()
            return _finish(nc, S)

        rmsnorm(1)
        for kc in range(16):
            for h in range(2):
                S.dma("sp", ag1_in.ap().rearrange("(k p) t -> p k t", p=128)[:, kc, h * 512:(h + 1) * 512],
                      hk(kc, h).ap, reads=hk(kc, h).toks, writes=[tk_ag1_in])
                S.dma("sp", xsp.ap().rearrange("p (k t) -> p k t", k=16)[:, kc, h * 512:(h + 1) * 512],
                      xk(kc, h).ap, reads=xk(kc, h).toks, writes=[tk_xsp])
        S.collective("AllGather", ALU.bypass, [ag1_in.ap().opt()], [ag1_out.ap().opt()],
                     reads=[tk_ag1_in], writes=[tk_ag1_out])
        S.fence()

        mixers(nc, S, locals())
        S.collective("AllGather", ALU.bypass, [ag2_in.ap().opt()], [ag2_out.ap().opt()],
                     reads=[tk_ag2_in], writes=[tk_ag2_out])
        S.fence()

        for kc in range(16):
            for h in range(2):
                S.dma("sp", xk(kc, h).ap,
                      xsp.ap().rearrange("p (k t) -> p k t", k=16)[:, kc, h * 512:(h + 1) * 512],
                      reads=[tk_xsp], writes=xk(kc, h).toks)
                S.dma("sp", hk(kc, h).ap,
                      ag1_in.ap().rearrange("(k p) t -> p k t", p=128)[:, kc, h * 512:(h + 1) * 512],
                      reads=[tk_ag1_in], writes=hk(kc, h).toks)
        if stop_after == "C":
            g2d = ag2_out.ap().rearrange("(h g p) t -> g p h t", g=2, p=128)
            for gi in range(2):
                for half in range(2):
                    def srcd(e, gi=gi, half=half):
                        return g2d[gi][:, :, bass.ds(S.off[half], 512)]
                    S.dma("sp", hT.h[:, gi * 8:gi * 8 + 8, half * 512:(half + 1) * 512], None,
                          reads=[tk_ag2_out], writes=[t for hh in range(8) for t in hk(gi * 8 + hh, half).toks],
                          in_fn=srcd, own_sem=True)
            for kc in range(16):
                for h in range(2):
                    OP("dve", "tensor_copy", xk(kc, h), hk(kc, h))
            write_out()
            return _finish(nc, S)
        R2.reset()
        oTg = R2.alloc([128, 8, 512], BF16)
        oTa = R2.alloc([128, 8, 512], BF16)
        mT = R2.alloc([128, 16, 512], BF16)
        g2 = ag2_out.ap().rearrange("(h g p) t -> g p h t", g=2, p=128)
        for half in range(2):
            for gi, dst in enumerate((oTg, oTa)):
                def src(e, gi=gi, half=half):
                    return g2[gi][:, :, bass.ds(S.off[half], 512)]
                S.dma("sp", dst.h[:, :, :], None, reads=[tk_ag2_out], writes=dst[:, :, :].toks,
                      in_fn=src, own_sem=True)
            for dc in range(16):
                s1, s2 = ring.next(), ring.next()
                DMA("pool", s1.v(s1.h[:, 0:2048]), V(wbr_d[dc], []))
                DMA("pool", s2[:, :], V(wgt_d[dc], []))
                wb = s1.h[:, 0:2048].rearrange("p (k g c) -> p k g c", k=8, g=2)
                wg = s2.h[:, :].rearrange("p (k g c) -> p k g c", k=16, g=2)
                p1, p2, p3, p4 = (allb.next() for _ in range(4))
                for kc in range(8):
                    MM(p1[:, :], s1.v(wb[:, kc, 0, :]), oTg.v(oTg.h[:, kc, :]), start=(kc == 0), stop=(kc == 7))
                for kc in range(16):
                    MM(p2[:, :], s2.v(wg[:, kc, 0, :]), hk(kc, half), start=(kc == 0), stop=(kc == 15))
                for kc in range(8):
                    MM(p3[:, :], s1.v(wb[:, kc, 1, :]), oTa.v(oTa.h[:, kc, :]), start=(kc == 0), stop=(kc == 7))
                for kc in range(16):
                    MM(p4[:, :], s2.v(wg[:, kc, 1, :]), hk(kc, half), start=(kc == 0), stop=(kc == 15))
                OP("act", "activation", tmpA[0][:, :], p2[:, :], AF.Sigmoid)
                OP("act", "activation", tmpA[1][:, :], p4[:, :], AF.Sigmoid)
                OP("dve", "tensor_tensor", tmpA[0][:, :], tmpA[0][:, :], p1[:, :], ALU.mult)
                OP("dve", "tensor_tensor", tmpA[1][:, :], tmpA[1][:, :], p3[:, :], ALU.mult)
                OP("dve", "tensor_tensor", mT.k(dc, mT.h[:, dc, :]), tmpA[0][:, :], tmpA[1][:, :], ALU.add)
            for dc in range(16):
                s1 = ring.next()
                DMA("pool", s1.v(s1.h[:, 0:2048]), V(wo_d[dc], []))
                w = s1.h[:, 0:2048].rearrange("p (k c) -> p k c", k=16)
                pd = allb.next()
                for kc in range(16):
                    MM(pd[:, :], s1.v(w[:, kc, :]), mT.k(kc, mT.h[:, kc, :]), start=(kc == 0), stop=(kc == 15))
                OP("dve", "tensor_tensor", xk(dc, half), xk(dc, half), pd[:, :], ALU.add)
        S.fence()
        if stop_after == "D":
            write_out()
            return _finish(nc, S)

        ffn(wgu_d[1], wd_d[1], 2)
        write_out()
        return _finish(nc, S)


def _finish(nc, S):
    with nc.Block() as block:
        S.emit(block)
    return nc


def mixers(nc, S, env):
    g = env
    OP, MM, TR, DMA = g["OP"], g["MM"], g["TR"], g["DMA"]
    R0, R1, R2 = g["R0"], g["R1"], g["R2"]
    banks, hvec, cw, bt, negA, epsc = g["banks"], g["hvec"], g["cw"], g["bt"], g["negA"], g["epsc"]
    ones_f, ident_f, tri_f, m_incl, m_strict, eye8 = (g[k] for k in
                                                      ("ones_f", "ident_f", "tri_f", "m_incl", "m_strict", "eye8"))
    ones_bf, ident_bf = g["ones_bf"], g["ident_bf"]
    ag1_out, ag2_in = g["ag1_out"], g["ag2_in"]
    tk_ag1_out, tk_ag2_in = g["tk_ag1_out"], g["tk_ag2_in"]
    wh_d = g["wh_d"]
    Rot = g["Rot"]
    R0.reset(); R1.reset(); R2.reset()
    NB = SEQ // 512

    class Reg2:
        def alloc(self, shape, dtype):
            try:
                return R0.alloc(shape, dtype)
            except AssertionError:
                return R2.alloc(shape, dtype)
    A = Reg2()
    wh = T(nc.alloc_sbuf_tensor_at("wh_sb", [128, 16, 898], BF16, offset=g["R3"].base))
    hb = [R1.alloc([128, 16, 512], BF16) for _ in range(2)]
    cb = [A.alloc([128, 515], F32) for _ in range(3)]
    cs = [A.alloc([128, 512], F32) for _ in range(3)]
    sqf = A.alloc([128, 512], F32)
    rs = A.alloc([128, 512], F32)
    gqT = A.alloc([128, 512], BF16)
    gkT = A.alloc([128, 512], BF16)
    gvb = A.alloc([128, 512], BF16)
    zs = A.alloc([128, 512], F32)
    arow = A.alloc([1, 512], F32)
    brow = A.alloc([1, 512], F32)
    Grow = A.alloc([1, 512], F32)
    onesrow = A.alloc([1, 64], F32)
    Gb = A.alloc([128, 512], F32)
    Bb = A.alloc([64, 512], F32)
    EGb = [A.alloc([128, 512], F32) for _ in range(2)]
    GBcol = A.alloc([64, 16], F32)
    EGcol = A.alloc([64, 8], F32)
    bexp = A.alloc([64, 8], F32)
    Dm = A.alloc([64, 8, 64], F32)
    Eq = A.alloc([64, 8, 64], F32)
    Ea = A.alloc([64, 8, 64], F32)
    Pm = [A.alloc([64, 8, 64], F32) for _ in range(2)]
    Qm = [A.alloc([64, 8, 64], F32) for _ in range(2)]
    Xm = A.alloc([64, 8, 64], F32)
    TTb = A.alloc([64, 8, 64], BF16)
    Aqk = [A.alloc([64, 8, 64], BF16) for _ in range(2)]
    vb = A.alloc([64, 8, 128], BF16)
    kbg = A.alloc([64, 8, 128], BF16)
    kdec = [A.alloc([64, 8, 128], BF16) for _ in range(2)]
    u = [A.alloc([64, 8, 128], F32) for _ in range(2)]
    wT = [A.alloc([128, 512], BF16) for _ in range(2)]
    qgT = [A.alloc([128, 512], BF16) for _ in range(2)]
    Sst = A.alloc([128, 128], F32)
    Sb = A.alloc([128, 128], BF16)
    vnew = [A.alloc([64, 128], BF16) for _ in range(2)]
    o32 = A.alloc([128, 512], F32)
    ogT = [A.alloc([128, 512], BF16) for _ in range(2)]
    oaT = [A.alloc([128, 512], BF16) for _ in range(2)]
    at0 = A.alloc([128, 512], F32)
    aqT = A.alloc([128, 512], BF16)
    kring = A.alloc([128, 8, 128], BF16)
    vring = A.alloc([128, 8, 128], BF16)
    avb = A.alloc([128, 512], BF16)
    tS = [A.alloc([128, 128], F32) for _ in range(2)]
    eT = [A.alloc([128, 128], BF16) for _ in range(2)]
    rden = A.alloc([128, 512], F32)

    rot = Rot(banks[0:5])
    psDen, psO, psAO = banks[5], banks[6], banks[7]
    SC = 128.0 ** -0.5

    whd = wh_d.rearrange("p (k c) -> p k c", k=16)
    for kc in range(16):
        DMA("pool", wh.v(wh.h[:, kc, :]), V(whd[:, kc, :], []))
    OP("dve", "memset", Sst[:, :], 0.0)
    OP("dve", "memset", Sb[:, :], 0.0)
    OP("dve", "memset", onesrow[:, :], 1.0)
    for i in range(3):
        OP("dve", "memset", cb[i].v(cb[i].h[:, 0:3]), 0.0)
    g1 = ag1_out.ap().rearrange("(s k p) t -> s p k t", s=NCORE, k=16)

    def c64(t, c):
        return t.v(t.h[:, c * 64:(c + 1) * 64])

    def proj(ps, j, hbt):
        for kc in range(16):
            MM(ps[:, :], wh.v(wh.h[:, kc, j * 128:(j + 1) * 128]), hbt.v(hbt.h[:, kc, :]),
               start=(kc == 0), stop=(kc == 15))

    def ssq_rstd(src, scale):
        OP("act", "activation", sqf[:, :], src, AF.Square)
        ps = rot.next()
        MM(ps[:, :], ones_f, sqf[:, :])
        OP("act", "activation", rs[:, :], ps[:, :], AF.Ln, scale=scale, bias=epsc.v(epsc.h[:, 0:1]))
        OP("act", "activation", rs[:, :], rs[:, :], AF.Exp, scale=-0.5)

    def prep(b):
        p2 = b % 2
        hbt = hb[p2]
        src, off = b // 2, (b % 2) * 512
        for kc in range(16):
            S.dma("sp", hbt.h[:, kc, :], g1[src][:, kc, off:off + 512], reads=[tk_ag1_out],
                  writes=hbt[:, :, :].toks)
        for i in range(3):
            ps = rot.next()
            proj(ps, i, hbt)
            if b > 0:
                OP("dve", "tensor_copy", cb[i].v(cb[i].h[:, 0:3]), cb[i].v(cb[i].h[:, 512:515]))
            OP("act", "activation", cb[i].v(cb[i].h[:, 3:515]), ps[:, :], AF.Copy)
            OP("dve", "tensor_scalar", cs[i][:, :], cb[i].v(cb[i].h[:, 3:515]),
               cw.v(cw.h[:, i * 4 + 3:i * 4 + 4]), None, ALU.mult)
            for tpp in (2, 1, 0):
                OP("dve", "scalar_tensor_tensor", cs[i][:, :], cb[i].v(cb[i].h[:, tpp:tpp + 512]),
                   cw.v(cw.h[:, i * 4 + tpp:i * 4 + tpp + 1]), cs[i][:, :], ALU.mult, ALU.add)
            OP("act", "activation", cs[i][:, :], cs[i][:, :], AF.Silu)
        yield
        ssq_rstd(cs[0][:, :], 1.0)
        OP("dve", "scalar_tensor_tensor", gqT[:, :], cs[0][:, :], SC, rs[:, :], ALU.mult, ALU.mult)
        ssq_rstd(cs[1][:, :], 1.0)
        OP("dve", "tensor_tensor", gkT[:, :], cs[1][:, :], rs[:, :], ALU.mult)
        OP("act", "activation", gvb[:, :], cs[2][:, :], AF.Copy)
        ps = rot.next()
        proj(ps, 3, hbt)
        OP("act", "activation", zs[:, :], ps[:, :], AF.Silu)
        yield
        psa, psb = rot.next(), rot.next()
        for kc in range(16):
            MM(psa.v(psa.h[0:1, :]), wh.v(wh.h[:, kc, 896:897]), hbt.v(hbt.h[:, kc, :]),
               start=(kc == 0), stop=(kc == 15))
        for kc in range(16):
            MM(psb.v(psb.h[0:1, :]), wh.v(wh.h[:, kc, 897:898]), hbt.v(hbt.h[:, kc, :]),
               start=(kc == 0), stop=(kc == 15))
        OP("act", "activation", arow[:, :], psa.v(psa.h[0:1, :]), AF.Exp, bias=hvec.v(hvec.h[0:1, 4:5]))
        OP("act", "activation", arow[:, :], arow[:, :], AF.Ln, bias=epsc.v(epsc.h[0:1, 1:2]))
        OP("dve", "tensor_scalar", arow[:, :], arow[:, :], negA.v(negA.h[0:1, 0:1]), None, ALU.mult)
        OP("act", "activation", brow[:, :], psb.v(psb.h[0:1, :]), AF.Exp, scale=-1.0)
        OP("dve", "tensor_scalar", brow[:, :], brow[:, :], 1.0, None, ALU.add)
        OP("dve", "reciprocal", brow[:, :], brow[:, :])
        for c in range(8):
            OP("dve", "tensor_tensor_scan", Grow.v(Grow.h[0:1, c * 64:(c + 1) * 64]), onesrow[:, :],
               arow.v(arow.h[0:1, c * 64:(c + 1) * 64]), 0.0, ALU.mult, ALU.add)
        psG, psB, psC = rot.next(), rot.next(), rot.next()
        MM(psG[:, :], V(ones_f.ap[0:1, :], ones_f.toks), Grow[:, :])
        MM(psB.v(psB.h[0:64, :]), V(ones_f.ap[0:1, 0:64], ones_f.toks), brow[:, :])
        for c in range(8):
            MM(psC.v(psC.h[0:64, c:c + 1]), Grow.v(Grow.h[0:1, c * 64:(c + 1) * 64]),
               V(ones_f.ap[0:1, 0:1], ones_f.toks))
            MM(psC.v(psC.h[0:64, 8 + c:9 + c]), brow.v(brow.h[0:1, c * 64:(c + 1) * 64]),
               V(ones_f.ap[0:1, 0:1], ones_f.toks))
        OP("act", "activation", Gb[:, :], psG[:, :], AF.Copy)
        OP("act", "activation", EGb[p2][:, :], psG[:, :], AF.Exp)
        OP("act", "activation", Bb[:, :], psB.v(psB.h[0:64, :]), AF.Copy)
        OP("act", "activation", GBcol[:, :], psC.v(psC.h[0:64, 0:16]), AF.Copy)
        OP("act", "activation", EGcol[:, :], GBcol.v(GBcol.h[:, 0:8]), AF.Exp)
        OP("dve", "tensor_tensor", bexp[:, :], EGcol[:, :], GBcol.v(GBcol.h[:, 8:16]), ALU.mult)
        for c in range(8):
            OP("dve", "tensor_scalar", Dm.v(Dm.h[:, c, :]), Gb.v(Gb.h[0:64, c * 64:(c + 1) * 64]),
               GBcol.v(GBcol.h[:, c:c + 1]), 0.0, ALU.subtract, ALU.min)
        OP("act", "activation", Dm[:, :, :], Dm[:, :, :], AF.Exp)
        f3 = lambda v_: V(v_.ap.rearrange("p (c i) -> p c i", c=8), v_.toks)
        OP("dve", "tensor_tensor", Eq[:, :, :], Dm[:, :, :], f3(m_incl), ALU.mult)
        OP("dve", "tensor_tensor", Ea[:, :, :], Dm[:, :, :], f3(m_strict), ALU.mult)
        OP("dve", "tensor_tensor", Ea[:, :, :], Ea[:, :, :],
           Bb.v(Bb.h[:, :].rearrange("p (c i) -> p c i", c=8)), ALU.mult)
        yield
        psK, psV = rot.next(), rot.next()
        pk = psK.h[0:64, :].bitcast(BF16).rearrange("p (c d) -> p c d", c=8)
        pv = psV.h[0:64, :].bitcast(BF16).rearrange("p (c d) -> p c d", c=8)
        for c in range(8):
            TR(psK.v(pk[:, c, :]), c64(gkT, c), ident_bf[:, :])
        for c in range(8):
            TR(psV.v(pv[:, c, :]), c64(gvb, c), ident_bf[:, :])
        bc = lambda ap: ap.unsqueeze(2).broadcast_to([64, 8, 128])
        OP("dve", "tensor_tensor", vb[:, :, :], psV.v(pv), GBcol.v(bc(GBcol.h[:, 8:16])), ALU.mult)
        OP("dve", "tensor_tensor", kbg[:, :, :], psK.v(pk), bexp.v(bc(bexp.h[:, :])), ALU.mult)
        OP("dve", "tensor_tensor", kdec[p2][:, :, :], psK.v(pk), Eq.v(bc(Eq.h[:, :, 63])), ALU.mult)
        psKK, psQK = rot.next(), rot.next()
        kk = psKK.h[0:64, :].rearrange("p (c i) -> p c i", c=8)
        qk = psQK.h[0:64, :].rearrange("p (c i) -> p c i", c=8)
        for c in range(8):
            MM(psKK.v(kk[:, c, :]), c64(gkT, c), c64(gkT, c))
        for c in range(8):
            MM(psQK.v(qk[:, c, :]), c64(gkT, c), c64(gqT, c))
        OP("dve", "tensor_tensor", Pm[0][:, :, :], psKK.v(kk), Ea[:, :, :], ALU.mult)
        OP("dve", "tensor_tensor", Aqk[p2][:, :, :], psQK.v(qk), Eq[:, :, :], ALU.mult)
        yield
        psT = rot.next()
        tq = psT.h[0:64, :].rearrange("p (c i) -> p c i", c=8)
        for c in range(8):
            TR(psT.v(tq[:, c, :]), Pm[0].v(Pm[0].h[:, c, :]), V(ident_f.ap[0:64, 0:64], ident_f.toks))
        OP("act", "activation", Qm[0][:, :, :], psT.v(tq), AF.Copy)
        OP("dve", "tensor_tensor", Xm[:, :, :], V(eye8.ap.rearrange("p (c i) -> p c i", c=8), eye8.toks),
           Pm[0][:, :, :], ALU.subtract)
        for lvl in range(1, 6):
            cur, nxt = (lvl - 1) % 2, lvl % 2
            if lvl <= 4:
                psP = rot.next()
                pp = psP.h[0:64, :].rearrange("p (c i) -> p c i", c=8)
                for c in range(8):
                    MM(psP.v(pp[:, c, :]), Qm[cur].v(Qm[cur].h[:, c, :]), Pm[cur].v(Pm[cur].h[:, c, :]))
            psQ = rot.next()
            pq = psQ.h[0:64, :].rearrange("p (c i) -> p c i", c=8)
            for c in range(8):
                MM(psQ.v(pq[:, c, :]), Pm[cur].v(Pm[cur].h[:, c, :]), Qm[cur].v(Qm[cur].h[:, c, :]))
            if lvl <= 4:
                OP("act", "activation", Pm[nxt][:, :, :], psP.v(pp), AF.Copy)
            OP("dve", "tensor_copy", Qm[nxt][:, :, :], psQ.v(pq))
            psX = rot.next()
            px = psX.h[0:64, :].rearrange("p (c i) -> p c i", c=8)
            for c in range(8):
                MM(psX.v(px[:, c, :]), Qm[nxt].v(Qm[nxt].h[:, c, :]), Xm.v(Xm.h[:, c, :]))
            OP("dve", "tensor_tensor", Xm[:, :, :], Xm[:, :, :], psX.v(px), ALU.add)
            yield
        OP("act", "activation", TTb[:, :, :], Xm[:, :, :], AF.Copy)
        for hf in range(2):
            psU = rot.next()
            pu_ = psU.h[0:64, :].rearrange("p (c e) -> p c e", c=4)
            for c in range(4):
                cc = hf * 4 + c
                MM(psU.v(pu_[:, c, :]), TTb.v(TTb.h[:, cc, :]), vb.v(vb.h[:, cc, :]))
            OP("act", "activation", u[p2].v(u[p2].h[:, hf * 4:hf * 4 + 4, :]), psU.v(pu_), AF.Copy)
        psW = rot.next()
        for c in range(8):
            MM(psW.v(psW.h[:, c * 64:(c + 1) * 64]), kbg.v(kbg.h[:, c, :]), TTb.v(TTb.h[:, c, :]))
        OP("act", "activation", wT[p2][:, :], psW[:, :], AF.Copy)
        OP("dve", "tensor_tensor", qgT[p2][:, :], gqT[:, :], EGb[p2][:, :], ALU.mult)
        yield

    def chain(b):
        p2 = b % 2
        for c in range(8):
            ps1 = rot.next()
            MM(ps1.v(ps1.h[0:64, 0:128]), c64(wT[p2], c), Sb[:, :])
            vn = vnew[c % 2]
            OP("dve", "tensor_tensor", vn[:, :], u[p2].v(u[p2].h[:, c, :]), ps1.v(ps1.h[0:64, 0:128]),
               ALU.subtract)
            MM(psO.v(psO.h[:, c * 64:(c + 1) * 64]), Sb[:, :], c64(qgT[p2], c), start=True, stop=False)
            MM(psO.v(psO.h[:, c * 64:(c + 1) * 64]), vn[:, :], Aqk[p2].v(Aqk[p2].h[:, c, :]),
               start=False, stop=True)
            ps4 = rot.next()
            MM(ps4.v(ps4.h[:, 0:128]), kdec[p2].v(kdec[p2].h[:, c, :]), vn[:, :])
            OP("dve", "scalar_tensor_tensor", Sst[:, :], Sst[:, :],
               EGb[p2].v(EGb[p2].h[:, c * 64 + 63:c * 64 + 64]), ps4.v(ps4.h[:, 0:128]), ALU.mult, ALU.add)
            OP("act", "activation", Sb[:, :], Sst[:, :], AF.Copy)
            yield
        OP("act", "activation", o32[:, :], psO[:, :], AF.Copy)
        ssq_rstd(o32[:, :], 1.0 / 128)
        OP("dve", "scalar_tensor_tensor", o32[:, :], o32[:, :], hvec.v(hvec.h[:, 0:1]), rs[:, :],
           ALU.mult, ALU.mult)
        OP("dve", "tensor_tensor", ogT[p2][:, :], o32[:, :], zs[:, :], ALU.mult)
        S.dma("sp", ag2_in.ap()[0:128, b * 512:(b + 1) * 512], ogT[p2].h[:, :],
              reads=ogT[p2][:, :].toks, writes=[tk_ag2_in])
        yield

    def attn(b):
        p2 = b % 2
        hbt = hb[p2]
        base = 4 * p2
        ps = rot.next()
        proj(ps, 4, hbt)
        OP("act", "activation", at0[:, :], ps[:, :], AF.Copy)
        ssq_rstd(at0[:, :], 1.0 / 128)
        OP("dve", "scalar_tensor_tensor", aqT[:, :], at0[:, :], hvec.v(hvec.h[:, 1:2]), rs[:, :],
           ALU.mult, ALU.mult)
        ps = rot.next()
        proj(ps, 5, hbt)
        OP("act", "activation", at0[:, :], ps[:, :], AF.Copy)
        ssq_rstd(at0[:, :], 1.0 / 128)
        OP("dve", "scalar_tensor_tensor",
           kring.v(kring.h[:, base:base + 4, :].rearrange("p m k -> p (m k)")), at0[:, :],
           hvec.v(hvec.h[:, 2:3]), rs[:, :], ALU.mult, ALU.mult)
        ps = rot.next()
        proj(ps, 6, hbt)
        OP("act", "activation", avb[:, :], ps[:, :], AF.Copy)
        psVt = rot.next()
        pvt = psVt.h[:, :].bitcast(BF16)[:, 0:512].rearrange("p (m e) -> p m e", m=4)
        for m in range(4):
            TR(psVt.v(pvt[:, m, :]), avb.v(avb.h[:, m * 128:(m + 1) * 128]), ident_bf[:, :])
        OP("dve", "tensor_copy", vring.v(vring.h[:, base:base + 4, :]), psVt.v(pvt))
        yield
        for m in range(4):
            pb = 4 * b + m
            rl = [r for r in range(5) if pb - 4 + r >= 0]
            for idx, r in enumerate(rl):
                sl = (pb - 4 + r) % 8
                psS = rot.next()
                MM(psS.v(psS.h[:, 0:128]), kring.v(kring.h[:, sl, :]), aqT.v(aqT.h[:, m * 128:(m + 1) * 128]))
                tt, ee = tS[idx % 2], eT[idx % 2]
                OP("dve", "scalar_tensor_tensor", tt[:, :], psS.v(psS.h[:, 0:128]), SC,
                   bt.v(bt.h[:, r, :]), ALU.mult, ALU.add)
                OP("act", "activation", ee[:, :], tt[:, :], AF.Exp)
                MM(psAO.v(psAO.h[:, m * 128:(m + 1) * 128]), vring.v(vring.h[:, sl, :]), ee[:, :],
                   start=(idx == 0), stop=(idx == len(rl) - 1), inc=True)
                MM(psDen.v(psDen.h[:, m * 128:(m + 1) * 128]), ones_bf[:, :], ee[:, :],
                   start=(idx == 0), stop=(idx == len(rl) - 1), inc=True)
            yield
        OP("dve", "reciprocal", rden[:, :], psDen[:, :])
        OP("dve", "tensor_tensor", oaT[p2][:, :], psAO[:, :], rden[:, :], ALU.mult)
        S.dma("sp", ag2_in.ap()[128:256, b * 512:(b + 1) * 512], oaT[p2].h[:, :],
              reads=oaT[p2][:, :].toks, writes=[tk_ag2_in])
        yield

    def drain(gen):
        for _ in gen:
            pass

    nb_run = int(os.environ.get("MK_NB", NB))
    skip = os.environ.get("MK_SKIP", "").split(",")
    for b in range(nb_run):
        if "prep" not in skip:
            drain(prep(b))
        if "attn" not in skip:
            drain(attn(b))
        if "chain" not in skip:
            drain(chain(b))


def _host_layout(inp):
    f = lambda a: np.ascontiguousarray(a, dtype=np.float32)
    x = inp["x"][0]
    nrm = np.stack([inp["ffn1_norm"][0], inp["mix_norm"][0], inp["ffn2_norm"][0]])
    nrm = f(nrm.reshape(3, 16, 128).transpose(2, 0, 1).reshape(128, 48))

    def gu(wg, wu):
        a = np.stack([wg.reshape(16, 128, NF, 128), wu.reshape(16, 128, NF, 128)], axis=3)
        return f(a.transpose(2, 1, 0, 3, 4).reshape(NF, 128, 4096))

    def dn(wd):
        w = np.zeros((44 * 128, D), np.float32)
        w[:5504] = wd
        w = w.reshape(2, 22, 128, 16, 128)
        return f(w.transpose(0, 3, 2, 1, 4).reshape(2, 16, 128, 22 * 128))

    shared = {
        "nrm": nrm,
        "wgu1": gu(inp["ffn1_w_gate"][0], inp["ffn1_w_up"][0]), "wd1": dn(inp["ffn1_w_down"][0]),
        "wgu2": gu(inp["ffn2_w_gate"][0], inp["ffn2_w_up"][0]), "wd2": dn(inp["ffn2_w_down"][0]),
    }
    w_in = inp["w_in"][0]
    gg = w_in[:, 7184:9232].reshape(16, 128, 16, 128)
    ga = w_in[:, 9232:11280].reshape(16, 128, 16, 128)
    shared["wgt"] = f(np.stack([gg, ga], axis=3).transpose(2, 1, 0, 3, 4).reshape(16, 128, 4096))
    bg = inp["w_branch_gdn"][0].reshape(8, 128, 16, 128)
    ba = inp["w_branch_att"][0].reshape(8, 128, 16, 128)
    shared["wbr"] = f(np.stack([bg, ba], axis=3).transpose(2, 1, 0, 3, 4).reshape(16, 128, 2048))
    shared["wo"] = f(inp["w_out"][0].reshape(16, 128, 16, 128).transpose(2, 1, 0, 3).reshape(16, 128, 2048))
    cst = np.zeros((128, 1856), np.float32)
    cst[:, 0:128] = 1.0
    cst[:, 128:256] = np.eye(128, dtype=np.float32)
    jj, ii = np.meshgrid(np.arange(64), np.arange(64), indexing="ij")
    cst[0:64, 256:320] = (jj <= ii)
    cst[0:64, 320:832] = np.tile((ii >= jj).astype(np.float32), (1, 8))
    cst[0:64, 832:1344] = np.tile((ii > jj).astype(np.float32), (1, 8))
    cst[0:64, 1344:1856] = np.tile(np.eye(64, dtype=np.float32), (1, 8))
    shared["cst"] = cst
    conv = inp["gdn_conv"][0]
    rel = inp["att_rel_bias"][0]
    maps = []
    for c in range(NCORE):
        m = dict(shared)
        xs = x[c * TC:(c + 1) * TC, :].T
        m["xT"] = f(xs.reshape(16, 128, TC).transpose(1, 0, 2).reshape(128, 16 * TC))
        cols = []
        for base in (0, 1024, 2048, 3072, 4112, 5136, 6160):
            cols.append(w_in[:, base + c * 128: base + (c + 1) * 128])
        cols.append(w_in[:, 4096 + c: 4097 + c])
        cols.append(w_in[:, 4104 + c: 4105 + c])
        whh = np.concatenate(cols, axis=1)
        m["wh"] = f(whh.reshape(16, 128, 898).transpose(1, 0, 2).reshape(128, 16 * 898))
        cwl = np.stack([conv[:, j * 1024 + c * 128: j * 1024 + (c + 1) * 128] for j in range(3)])
        m["cw"] = f(cwl.transpose(2, 0, 1).reshape(128, 12))
        hv = np.zeros((128, 8), np.float32)
        hv[:, 0] = inp["gdn_out_norm"][0]
        hv[:, 1] = inp["att_q_norm"][0]
        hv[:, 2] = inp["att_k_norm"][0]
        hv[:, 3] = inp["gdn_A_log"][0, c]
        hv[:, 4] = inp["gdn_dt_bias"][0, c]
        m["hvec"] = hv
        j = np.arange(128)[:, None]
        i = np.arange(128)[None, :]
        tiles = []
        for r in range(5):
            idx = np.clip(128 * (4 - r) + i - j, -63, 256) + 63
            tiles.append(rel[c][idx])
        m["bt"] = f(np.stack(tiles, axis=1).reshape(128, 5 * 128))
        maps.append(m)
    return maps


_NC_CACHE = {}


def kernel(**inputs):
    inp = {k: np.asarray(v) for k, v in inputs.items()}
    stop = os.environ.get("MK_STOP") or None
    if stop not in _NC_CACHE:
        _NC_CACHE[stop] = build(stop)
    nc = _NC_CACHE[stop]
    maps = _host_layout(inp)
    res = run_bass_kernel_spmd(nc, maps, core_ids=list(range(NCORE)))
    out = np.empty((1, SEQ, D), np.float32)
    for c in range(NCORE):
        y = np.asarray(res.results[c]["yT"]).reshape(128, 16, TC)
        out[0, c * TC:(c + 1) * TC, :] = y.transpose(2, 1, 0).reshape(TC, D)
    return out
```

```python
import contextlib
import os
import numpy as np
import concourse.bass as bass
import concourse.mybir as mybir
from concourse.bass_utils import run_bass_kernel_spmd

F32 = mybir.dt.float32
BF16 = mybir.dt.bfloat16
AF = mybir.ActivationFunctionType
ALU = mybir.AluOpType

NCORE = 8
D = 2048
SEQ = 8192
TC = 1024
NF = 43
EPS = 1e-6
SB_BASE = 16512
SB_END = 229376
NEG = -30000.0


class Tok:
    __slots__ = ("w", "r")

    def __init__(self):
        self.w = None
        self.r = {}


class V:
    __slots__ = ("ap", "toks")

    def __init__(self, ap, toks):
        self.ap = ap
        self.toks = toks


class T:
    def __init__(self, handle):
        self.h = handle
        self.toks = {}

    def k(self, key, ap):
        if key not in self.toks:
            self.toks[key] = Tok()
        return V(ap, [self.toks[key]])

    def all(self, ap):
        return V(ap, list(self.toks.values()))

    def __getitem__(self, idx):
        return self.k(None, self.h[idx])

    def v(self, ap):
        return self.k(None, ap)


class Sched:
    ENGS = ("pe", "act", "dve", "pool", "sp")

    def __init__(self, nc, stack, n_dma_sems=24):
        self.nc = nc
        self.stack = stack
        self.ops = {e: [] for e in self.ENGS}
        self.cnt = {e: 0 for e in self.ENGS}
        self.waited = {e: {} for e in self.ENGS}
        self.sems = {}
        for e in self.ENGS:
            self.sems[e] = stack.enter_context(nc.semaphore("s_" + e))
        self.dma_sems = {"sp": [], "pool": []}
        self.dma_val = {}
        for q, n in (("sp", 16), ("pool", 8)):
            for i in range(n):
                k = "d%s%d" % (q, i)
                self.sems[k] = stack.enter_context(nc.semaphore("s_" + k))
                self.dma_sems[q].append(k)
                self.dma_val[k] = 0
        self.dma_rr = {"sp": 0, "pool": 0}
        self.n_cc = 0

    def _collect(self, eng, reads, writes):
        waits = {}

        def need(ev):
            if ev is None:
                return
            k, v = ev
            if eng == "pe" and k == "pe":
                return
            if self.waited[eng].get(k, 0) >= v:
                return
            if waits.get(k, 0) < v:
                waits[k] = v

        for t in reads:
            need(t.w)
        for t in writes:
            need(t.w)
            for k, v in t.r.items():
                need((k, v))
        for k, v in waits.items():
            self.waited[eng][k] = v
        return waits

    def _mark(self, ev, reads, writes):
        k, v = ev
        for t in reads:
            if t.r.get(k, 0) < v:
                t.r[k] = v
        for t in writes:
            t.w = ev
            t.r = {}

    def op(self, eng, fn, reads=(), writes=(), inc=True):
        waits = self._collect(eng, reads, writes)
        if inc:
            self.cnt[eng] += 1
            ev = (eng, self.cnt[eng])
        else:
            assert eng == "pe"
            ev = (eng, self.cnt[eng] + 1)
        self._mark(ev, reads, writes)
        self.ops[eng].append((list(waits.items()), fn, (eng, 1) if inc else None))

    def dma(self, q, out, in_, reads=(), writes=(), in_fn=None, own_sem=False):
        waits = self._collect(q, reads, writes)
        if own_sem:
            k = "x%d" % self.n_cc
            self.n_cc += 1
            self.sems[k] = self.stack.enter_context(self.nc.semaphore("s_" + k))
            self.dma_val[k] = 0
        else:
            k = self.dma_sems[q][self.dma_rr[q]]
            self.dma_rr[q] = (self.dma_rr[q] + 1) % len(self.dma_sems[q])
        if self.dma_val[k] > 0 and self.waited[q].get(k, 0) < self.dma_val[k]:
            waits[k] = self.dma_val[k]
            self.waited[q][k] = self.dma_val[k]
        self.dma_val[k] += 16
        ev = (k, self.dma_val[k])
        self._mark(ev, reads, writes)
        if in_fn is None:
            fn = lambda e: e.dma_start(out=out, in_=in_)
        else:
            fn = lambda e: e.dma_start(out=out, in_=in_fn(e))
        self.ops[q].append((list(waits.items()), fn, (k, 16)))

    def collective(self, kind, alu, ins, outs, reads=(), writes=()):
        q = "pool"
        waits = self._collect(q, reads, writes)
        k = "cc%d" % self.n_cc
        self.n_cc += 1
        self.sems[k] = self.stack.enter_context(self.nc.semaphore("s_" + k))
        ev = (k, 1)
        self._mark(ev, reads, writes)

        def fn(e):
            return e.collective_compute(kind, alu, replica_groups=[list(range(NCORE))],
                                        ins=ins, outs=outs)
        self.ops[q].append((list(waits.items()), fn, (k, 1)))

    def fence(self, engs=("pe", "act", "dve", "sp")):
        snap = {e: self.cnt[e] for e in engs}
        dsnap = dict(self.dma_val)
        for e in engs:
            waits = {}
            for o in engs:
                if o != e and self.waited[e].get(o, 0) < snap[o]:
                    waits[o] = snap[o]
            for k, v in dsnap.items():
                if v > 0 and self.waited[e].get(k, 0) < v:
                    waits[k] = v
            for k, v in waits.items():
                self.waited[e][k] = v
            self.ops[e].append((list(waits.items()), None, None))

    def finish(self, toks):
        waits = self._collect("sp", [], toks)
        self.ops["sp"].append((list(waits.items()), None, None))

    def check(self):
        sem = {}
        pc = {e: 0 for e in self.ENGS}
        progress = True
        while progress:
            progress = False
            for e in self.ENGS:
                while pc[e] < len(self.ops[e]):
                    waits, fn, inc = self.ops[e][pc[e]]
                    if all(sem.get(k, 0) >= v for k, v in waits):
                        if fn is not None and inc is not None:
                            sem[inc[0]] = sem.get(inc[0], 0) + inc[1]
                        pc[e] += 1
                        progress = True
                    else:
                        break
        stuck = {e: (pc[e], len(self.ops[e]), self.ops[e][pc[e]][0]) for e in self.ENGS
                 if pc[e] < len(self.ops[e])}
        assert not stuck, stuck

    def emit(self, block):
        self.check()
        sems = self.sems

        def run(name):
            def f(e):
                if name == "sp":
                    self.pid = e.partition_id()
                    self.off = [e.snap(self.pid * TC + h * 512, min_val=0, max_val=SEQ - 512) for h in range(2)]
                for waits, fn, inc in self.ops[name]:
                    for k, v in waits:
                        e.wait_ge(sems[k], v)
                    if fn is None:
                        continue
                    try:
                        ins = fn(e)
                    except Exception:
                        print("EMIT FAIL", name, self.ops[name].index((waits, fn, inc)), len(self.ops[name]), waits, inc)
                        import traceback; traceback.print_exc()
                        try:
                            fn(e); print("RETRY OK")
                        except Exception as ex2:
                            print("RETRY FAIL", ex2)
                        raise
                    if inc is not None:
                        ins.then_inc(sems[inc[0]], inc[1])
            return f

        block.tensor(run("pe"))
        block.scalar(run("act"))
        block.vector(run("dve"))
        block.gpsimd(run("pool"))
        block.sync(run("sp"))


class Region:
    def __init__(self, nc, name, base, size):
        self.nc, self.name, self.base, self.size, self.off, self.n = nc, name, base, size, 0, 0

    def reset(self):
        self.off = 0

    def alloc(self, shape, dtype):
        nb = int(np.prod(shape[1:])) * (4 if dtype == F32 else 2)
        nb = (nb + 31) // 32 * 32
        assert self.off + nb <= self.size, (self.name, self.off, nb, self.size)
        h = self.nc.alloc_sbuf_tensor_at("%s_%d" % (self.name, self.n), list(shape), dtype,
                                         offset=self.base + self.off)
        self.off += nb
        self.n += 1
        return T(h)


def build(stop_after=None):
    nc = bass.Bass("TRN2", target_bir_lowering=False)
    dram_in = lambda n, s, dt=F32: nc.dram_tensor(n, s, dt, kind="ExternalInput").ap()
    xT_d = dram_in("xT", [128, 16 * TC])
    nrm_d = dram_in("nrm", [128, 48])
    wgu_d = [dram_in("wgu%d" % i, [NF, 128, 4096]) for i in (1, 2)]
    wd_d = [dram_in("wd%d" % i, [2, 16, 128, 22 * 128]) for i in (1, 2)]
    wh_d = dram_in("wh", [128, 16 * 898])
    wgt_d = dram_in("wgt", [16, 128, 4096])
    wbr_d = dram_in("wbr", [16, 128, 2048])
    wo_d = dram_in("wo", [16, 128, 2048])
    cw_d = dram_in("cw", [128, 12])
    hvec_d = dram_in("hvec", [128, 8])
    bt_d = dram_in("bt", [128, 5 * 128])
    cst_d = dram_in("cst", [128, 128 + 128 + 64 + 512 + 512 + 512])
    yT_d = nc.dram_tensor("yT", [128, 16 * TC], F32, kind="ExternalOutput").ap()
    ag1_in = [nc.dram_tensor("ag1_in%d" % h, [D, 512], BF16) for h in range(2)]
    ag1_out = [nc.dram_tensor("ag1_out%d" % h, [NCORE * D, 512], BF16) for h in range(2)]
    ag2_in = nc.dram_tensor("ag2_in", [256, SEQ], BF16)
    ag2_out = nc.dram_tensor("ag2_out", [NCORE * 256, SEQ], BF16)
    xsp = nc.dram_tensor("xsp", [128, 16 * TC], F32)
    tk_ag1_out, tk_ag2_out = [Tok(), Tok()], Tok()
    tk_ag1_in, tk_xsp, tk_y, tk_ag2_in = {}, {}, [], []
    def newtok(lst):
        t = Tok()
        lst.append(t)
        return t

    with contextlib.ExitStack() as st:
        S = Sched(nc, st)
        R0 = Region(nc, "r0", SB_BASE, 65536)
        R1 = Region(nc, "r1", SB_BASE + 65536, 32768)
        R2 = Region(nc, "r2", SB_BASE + 98304, 45056)
        R3 = Region(nc, "r3", SB_BASE + 143360, 32768)
        R4 = Region(nc, "r4", SB_BASE + 176128, SB_END - SB_BASE - 176128)

        banks = [T(st.enter_context(nc.psum_tensor("ps%d" % i, [128, 512], F32))) for i in range(8)]

        class Rot:
            def __init__(self, items):
                self.items, self.i = items, 0

            def next(self):
                x = self.items[self.i % len(self.items)]
                self.i += 1
                return x

        def OP(eng, method, out, *args, **kw):
            vs = [a for a in list(args) + list(kw.values()) if isinstance(a, V)]

            def fn(e):
                a2 = [a.ap if isinstance(a, V) else a for a in args]
                k2 = {k: (v.ap if isinstance(v, V) else v) for k, v in kw.items()}
                return getattr(e, method)(out.ap, *a2, **k2)
            reads = [t for v in vs for t in v.toks]
            S.op(eng, fn, reads=reads, writes=out.toks)

        def MM(out, lhsT, rhs, start=True, stop=True, inc=None):
            def fn(e):
                return e.matmul(out.ap, lhsT=lhsT.ap, rhs=rhs.ap, start=start, stop=stop)
            S.op("pe", fn, reads=lhsT.toks + rhs.toks, writes=out.toks, inc=(stop if inc is None else inc))

        def TR(out, in_, ident):
            def fn(e):
                return e.transpose(out.ap, in_.ap, ident.ap)
            S.op("pe", fn, reads=in_.toks + ident.toks, writes=out.toks, inc=True)

        def DMA(q, out, in_, in_fn=None):
            S.dma(q, out.ap, in_.ap, reads=in_.toks, writes=out.toks, in_fn=in_fn)

        xT = T(nc.alloc_sbuf_tensor_at("xT_sb", [128, 16, TC], F32, offset=R0.base))
        hT = T(nc.alloc_sbuf_tensor_at("hT_sb", [128, 16, TC], BF16, offset=R1.base))
        actT = T(nc.alloc_sbuf_tensor_at("actT_sb", [128, 22, TC], BF16, offset=R2.base))
        ring = Rot([T(nc.alloc_sbuf_tensor_at("ring%d" % i, [128, 4096], BF16,
                                              offset=R3.base + i * 8192)) for i in range(4)])
        nrm = R4.alloc([128, 48], F32)
        cw = R4.alloc([128, 12], F32)
        hvec = R4.alloc([128, 8], F32)
        bt = R4.alloc([128, 5, 128], F32)
        cst = R4.alloc([128, 1856], F32)
        ones_bf = R4.alloc([128, 128], BF16)
        ident_bf = R4.alloc([128, 128], BF16)
        rstd = R4.alloc([128, TC], F32)
        tmpA = [R4.alloc([128, 512], F32) for _ in range(2)]
        sqb = [R4.alloc([128, 512], BF16) for _ in range(2)]
        negA = R4.alloc([128, 1], F32)
        epsc = R4.alloc([128, 2], F32)
        ones_f = cst.v(cst.h[:, 0:128])
        ident_f = cst.v(cst.h[:, 128:256])
        tri_f = cst.v(cst.h[0:64, 256:320])
        m_incl = cst.v(cst.h[0:64, 320:832])
        m_strict = cst.v(cst.h[0:64, 832:1344])
        eye8 = cst.v(cst.h[0:64, 1344:1856])

        def xk(kc, h):
            return xT.k((kc, h), xT.h[:, kc, h * 512:(h + 1) * 512])

        def hk(kc, h):
            return hT.k((kc, h), hT.h[:, kc, h * 512:(h + 1) * 512])

        def ak(f, h):
            return actT.k((f, h), actT.h[:, f, h * 512:(h + 1) * 512])

        for i in range(4):
            for h in range(2):
                pass
        for kc in range(16):
            for h in range(2):
                DMA("sp", xk(kc, h),
                    V(xT_d.rearrange("p (k t) -> p k t", k=16)[:, kc, h * 512:(h + 1) * 512], []))
        DMA("sp", nrm[:, :], V(nrm_d, []))
        DMA("sp", cw[:, :], V(cw_d, []))
        DMA("sp", hvec[:, :], V(hvec_d, []))
        DMA("sp", bt.v(bt.h[:, :, :]), V(bt_d.rearrange("p (r i) -> p r i", r=5), []))
        DMA("sp", cst[:, :], V(cst_d, []))
        OP("dve", "tensor_copy", ones_bf[:, :], ones_f)
        OP("dve", "tensor_copy", ident_bf[:, :], ident_f)
        OP("dve", "memset", bt.v(bt.h[0:64, 0, 64:128]), NEG)
        OP("dve", "memset", bt.v(bt.h[64:128, 4, 0:64]), NEG)
        OP("dve", "memset", epsc.v(epsc.h[:, 0:1]), EPS)
        OP("dve", "memset", epsc.v(epsc.h[:, 1:2]), 1.0)
        OP("act", "activation", negA[:, :], hvec.v(hvec.h[:, 3:4]), AF.Exp)
        OP("dve", "tensor_scalar", negA[:, :], negA[:, :], -1.0, None, ALU.mult)

        allb = Rot(banks)

        def rmsnorm(n_idx):
            for h in range(2):
                ps = allb.next()
                for kc in range(16):
                    sq = sqb[kc % 2]
                    OP("act", "activation", sq[:, :], xk(kc, h), AF.Square)
                    MM(ps[:, :], ones_bf[:, :], sq[:, :], start=(kc == 0), stop=(kc == 15), inc=True)
                rh = rstd.k(h, rstd.h[:, h * 512:(h + 1) * 512])
                OP("act", "activation", rh, ps[:, :], AF.Ln, scale=1.0 / D, bias=epsc.v(epsc.h[:, 0:1]))
                OP("act", "activation", rh, rh, AF.Exp, scale=-0.5)
                for kc in range(16):
                    OP("dve", "scalar_tensor_tensor", hk(kc, h), xk(kc, h),
                       nrm.v(nrm.h[:, n_idx * 16 + kc:n_idx * 16 + kc + 1]), rh, ALU.mult, ALU.mult)

        def ffn(wgu, wd, n_idx):
            rmsnorm(n_idx)
            for fh in range(2):
                nf = 22 if fh == 0 else 21
                if os.environ.get("MK_FAST"):
                    nf = int(os.environ["MK_FAST"])
                for fl in range(nf):
                    f = fh * 22 + fl
                    slot = ring.next()
                    DMA("pool", slot[:, :], V(wgu[f], []))
                    w = slot.h[:, :].rearrange("p (k g c) -> p k g c", k=16, g=2)
                    for h in range(2):
                        pg, pu = allb.next(), allb.next()
                        for kc in range(16):
                            MM(pg[:, :], slot.v(w[:, kc, 0, :]), hk(kc, h), start=(kc == 0), stop=(kc == 15))
                        for kc in range(16):
                            MM(pu[:, :], slot.v(w[:, kc, 1, :]), hk(kc, h), start=(kc == 0), stop=(kc == 15))
                        tmp = tmpA[(fl * 2 + h) % 2]
                        OP("act", "activation", tmp[:, :], pg[:, :], AF.Silu)
                        OP("dve", "tensor_tensor", ak(fl, h), tmp[:, :], pu[:, :], ALU.mult)
                for dc in range(16 if not os.environ.get("MK_FAST") else 2):
                    slot = ring.next()
                    DMA("pool", slot.v(slot.h[:, 0:nf * 128]), V(wd[fh, dc][:, 0:nf * 128], []))
                    w = slot.h[:, 0:22 * 128].rearrange("p (f c) -> p f c", f=22)
                    for h in range(2):
                        pd = allb.next()
                        for fl in range(nf):
                            MM(pd[:, :], slot.v(w[:, fl, :]), ak(fl, h), start=(fl == 0), stop=(fl == nf - 1))
                        OP("dve", "scalar_tensor_tensor", xk(dc, h), pd[:, :], 0.5, xk(dc, h),
                           ALU.mult, ALU.add)

        def write_out():
            for kc in range(16):
                for h in range(2):
                    S.dma("sp", yT_d.rearrange("p (k t) -> p k t", k=16)[:, kc, h * 512:(h + 1) * 512],
                          xk(kc, h).ap, reads=xk(kc, h).toks, writes=[newtok(tk_y)])
            S.finish(tk_y)

        if not os.environ.get("MK_NOFFN"):
            ffn(wgu_d[0], wd_d[0], 0)
        if stop_after == "A":
            write_out()
            return _finish(nc, S)

        rmsnorm(1)
        for h in range(2):
            for kc in range(16):
                S.dma("sp", ag1_in[h].ap().rearrange("(k p) t -> p k t", p=128)[:, kc, :],
                      hk(kc, h).ap, reads=hk(kc, h).toks, writes=[tk_ag1_in.setdefault((kc, h), Tok())])
            S.collective("AllGather", ALU.bypass, [ag1_in[h].ap().opt()], [ag1_out[h].ap().opt()],
                         reads=[tk_ag1_in[(kc, h)] for kc in range(16)], writes=[tk_ag1_out[h]])
        for kc in range(16):
            for h in range(2):
                S.dma("sp", xsp.ap().rearrange("p (k t) -> p k t", k=16)[:, kc, h * 512:(h + 1) * 512],
                      xk(kc, h).ap, reads=xk(kc, h).toks, writes=[tk_xsp.setdefault((kc, h), Tok())])
        S.fence()

        mixers(nc, S, locals())
        S.collective("AllGather", ALU.bypass, [ag2_in.ap().opt()], [ag2_out.ap().opt()],
                     reads=tk_ag2_in, writes=[tk_ag2_out])
        S.fence()

        for kc in range(16):
            for h in range(2):
                S.dma("sp", xk(kc, h).ap,
                      xsp.ap().rearrange("p (k t) -> p k t", k=16)[:, kc, h * 512:(h + 1) * 512],
                      reads=[tk_xsp[(kc, h)]], writes=xk(kc, h).toks)
                S.dma("sp", hk(kc, h).ap,
                      ag1_in[h].ap().rearrange("(k p) t -> p k t", p=128)[:, kc, :],
                      reads=[tk_ag1_in[(kc, h)]], writes=hk(kc, h).toks)
        if stop_after == "C":
            g2d = ag2_out.ap().rearrange("(h g p) t -> g p h t", g=2, p=128)
            for gi in range(2):
                for half in range(2):
                    def srcd(e, gi=gi, half=half):
                        return g2d[gi][:, :, bass.ds(S.off[half], 512)]
                    S.dma("sp", hT.h[:, gi * 8:gi * 8 + 8, half * 512:(half + 1) * 512], None,
                          reads=[tk_ag2_out], writes=[t for hh in range(8) for t in hk(gi * 8 + hh, half).toks],
                          in_fn=srcd, own_sem=True)
            for kc in range(16):
                for h in range(2):
                    OP("dve", "tensor_copy", xk(kc, h), hk(kc, h))
            write_out()
            return _finish(nc, S)
        R2.reset()
        oTg = R2.alloc([128, 8, 512], BF16)
        oTa = R2.alloc([128, 8, 512], BF16)
        mT = R2.alloc([128, 16, 512], BF16)
        g2 = ag2_out.ap().rearrange("(h g p) t -> g p h t", g=2, p=128)
        for half in range(2):
            for gi, dst in enumerate((oTg, oTa)):
                def src(e, gi=gi, half=half):
                    return g2[gi][:, :, bass.ds(S.off[half], 512)]
                S.dma("sp", dst.h[:, :, :], None, reads=[tk_ag2_out], writes=dst[:, :, :].toks,
                      in_fn=src, own_sem=True)
            for dc in range(16):
                s1, s2 = ring.next(), ring.next()
                DMA("pool", s1.v(s1.h[:, 0:2048]), V(wbr_d[dc], []))
                DMA("pool", s2[:, :], V(wgt_d[dc], []))
                wb = s1.h[:, 0:2048].rearrange("p (k g c) -> p k g c", k=8, g=2)
                wg = s2.h[:, :].rearrange("p (k g c) -> p k g c", k=16, g=2)
                p1, p2, p3, p4 = (allb.next() for _ in range(4))
                for kc in range(8):
                    MM(p1[:, :], s1.v(wb[:, kc, 0, :]), oTg.v(oTg.h[:, kc, :]), start=(kc == 0), stop=(kc == 7))
                for kc in range(16):
                    MM(p2[:, :], s2.v(wg[:, kc, 0, :]), hk(kc, half), start=(kc == 0), stop=(kc == 15))
                for kc in range(8):
                    MM(p3[:, :], s1.v(wb[:, kc, 1, :]), oTa.v(oTa.h[:, kc, :]), start=(kc == 0), stop=(kc == 7))
                for kc in range(16):
                    MM(p4[:, :], s2.v(wg[:, kc, 1, :]), hk(kc, half), start=(kc == 0), stop=(kc == 15))
                OP("act", "activation", tmpA[0][:, :], p2[:, :], AF.Sigmoid)
                OP("act", "activation", tmpA[1][:, :], p4[:, :], AF.Sigmoid)
                OP("dve", "tensor_tensor", tmpA[0][:, :], tmpA[0][:, :], p1[:, :], ALU.mult)
                OP("dve", "tensor_tensor", tmpA[1][:, :], tmpA[1][:, :], p3[:, :], ALU.mult)
                OP("dve", "tensor_tensor", mT.k(dc, mT.h[:, dc, :]), tmpA[0][:, :], tmpA[1][:, :], ALU.add)
            for dc in range(16):
                s1 = ring.next()
                DMA("pool", s1.v(s1.h[:, 0:2048]), V(wo_d[dc], []))
                w = s1.h[:, 0:2048].rearrange("p (k c) -> p k c", k=16)
                pd = allb.next()
                for kc in range(16):
                    MM(pd[:, :], s1.v(w[:, kc, :]), mT.k(kc, mT.h[:, kc, :]), start=(kc == 0), stop=(kc == 15))
                OP("dve", "tensor_tensor", xk(dc, half), xk(dc, half), pd[:, :], ALU.add)
        S.fence()
        if stop_after == "D":
            write_out()
            return _finish(nc, S)

        ffn(wgu_d[1], wd_d[1], 2)
        write_out()
        return _finish(nc, S)


def _finish(nc, S):
    with nc.Block() as block:
        S.emit(block)
    return nc


def mixers(nc, S, env):
    g = env
    OP, MM, TR, DMA = g["OP"], g["MM"], g["TR"], g["DMA"]
    R0, R1, R2 = g["R0"], g["R1"], g["R2"]
    banks, hvec, cw, bt, negA, epsc = g["banks"], g["hvec"], g["cw"], g["bt"], g["negA"], g["epsc"]
    ones_f, ident_f, tri_f, m_incl, m_strict, eye8 = (g[k] for k in
                                                      ("ones_f", "ident_f", "tri_f", "m_incl", "m_strict", "eye8"))
    ones_bf, ident_bf = g["ones_bf"], g["ident_bf"]
    ag1_out, ag2_in = g["ag1_out"], g["ag2_in"]
    tk_ag1_out, tk_ag2_in = g["tk_ag1_out"], g["tk_ag2_in"]
    wh_d = g["wh_d"]
    Rot = g["Rot"]
    R0.reset(); R1.reset(); R2.reset()
    NB = SEQ // 512

    class Reg2:
        def alloc(self, shape, dtype):
            try:
                return R0.alloc(shape, dtype)
            except AssertionError:
                return R2.alloc(shape, dtype)
    A = Reg2()
    wh = T(nc.alloc_sbuf_tensor_at("wh_sb", [128, 16, 898], BF16, offset=g["R3"].base))
    ring_toks = [sl[:, :].toks[0] for sl in g["ring"].items]
    wh.v = lambda ap: V(ap, ring_toks)
    hb = [R1.alloc([128, 16, 512], BF16) for _ in range(2)]
    cb = [A.alloc([128, 515], F32) for _ in range(3)]
    cs = [A.alloc([128, 512], F32) for _ in range(3)]
    sqf3 = [A.alloc([128, 512], F32) for _ in range(3)]
    rs3 = [A.alloc([128, 512], F32) for _ in range(3)]
    gqT = A.alloc([128, 512], BF16)
    gkT = A.alloc([128, 512], BF16)
    gvb = A.alloc([128, 512], BF16)
    zs = [A.alloc([128, 512], F32) for _ in range(2)]
    arow = A.alloc([1, 512], F32)
    brow = A.alloc([1, 512], F32)
    Grow = A.alloc([1, 512], F32)
    onesrow = A.alloc([1, 64], F32)
    Gb = A.alloc([128, 512], F32)
    Bb = A.alloc([64, 512], F32)
    EGb = [A.alloc([128, 512], F32) for _ in range(2)]
    GBcol = A.alloc([64, 16], F32)
    EGcol = A.alloc([64, 8], F32)
    bexp = A.alloc([64, 8], F32)
    Dm = A.alloc([64, 8, 64], F32)
    Eq = A.alloc([64, 8, 64], F32)
    Ea = A.alloc([64, 8, 64], F32)
    Pm = [A.alloc([64, 8, 64], F32) for _ in range(2)]
    Qm = [A.alloc([64, 8, 64], F32) for _ in range(2)]
    Xm = A.alloc([64, 8, 64], F32)
    TTb = A.alloc([64, 8, 64], BF16)
    Aqk = [A.alloc([64, 8, 64], BF16) for _ in range(2)]
    vb = A.alloc([64, 8, 128], BF16)
    kbg = A.alloc([64, 8, 128], BF16)
    kdec = [A.alloc([64, 8, 128], BF16) for _ in range(2)]
    u = [A.alloc([64, 8, 128], F32) for _ in range(2)]
    wT = [A.alloc([128, 512], BF16) for _ in range(2)]
    qgT = [A.alloc([128, 512], BF16) for _ in range(2)]
    Sst = A.alloc([128, 128], F32)
    Sb = A.alloc([128, 128], BF16)
    vnew = [A.alloc([64, 128], BF16) for _ in range(2)]
    o32 = A.alloc([128, 512], F32)
    ogT = [A.alloc([128, 512], BF16) for _ in range(2)]
    oaT = [A.alloc([128, 512], BF16) for _ in range(2)]
    at0 = A.alloc([128, 512], F32)
    aqT = A.alloc([128, 512], BF16)
    kring = A.alloc([128, 8, 128], BF16)
    vring = A.alloc([128, 8, 128], BF16)
    avb = A.alloc([128, 512], BF16)
    tS = [A.alloc([128, 128], F32) for _ in range(2)]
    eT = [A.alloc([128, 128], BF16) for _ in range(2)]
    rden = A.alloc([128, 512], F32)

    rot = Rot(banks[0:5])
    psDen, psO, psAO = banks[5], banks[6], banks[7]
    SC = 128.0 ** -0.5

    whd = wh_d.rearrange("p (k c) -> p k c", k=16)
    for kc in range(16):
        DMA("pool", wh.v(wh.h[:, kc, :]), V(whd[:, kc, :], []))
    OP("dve", "memset", Sst[:, :], 0.0)
    OP("dve", "memset", Sb[:, :], 0.0)
    OP("dve", "memset", onesrow[:, :], 1.0)
    for i in range(3):
        OP("dve", "memset", cb[i].v(cb[i].h[:, 0:3]), 0.0)
    g1 = [ag1_out[h].ap().rearrange("(s k p) t -> s p k t", s=NCORE, k=16) for h in range(2)]

    def c64(t, c):
        return t.v(t.h[:, c * 64:(c + 1) * 64])

    def proj(ps, j, hbt):
        for kc in range(16):
            MM(ps[:, :], wh.v(wh.h[:, kc, j * 128:(j + 1) * 128]), hbt.k(kc, hbt.h[:, kc, :]),
               start=(kc == 0), stop=(kc == 15))

    def ssq_rstd(src, scale, si):
        sqf, rs = sqf3[si], rs3[si]
        OP("act", "activation", sqf[:, :], src, AF.Square)
        ps = rot.next()
        MM(ps[:, :], ones_f, sqf[:, :])
        OP("act", "activation", rs[:, :], ps[:, :], AF.Ln, scale=scale, bias=epsc.v(epsc.h[:, 0:1]))
        OP("act", "activation", rs[:, :], rs[:, :], AF.Exp, scale=-0.5)
        return rs

    def prep(b):
        p2 = b % 2
        hbt = hb[p2]
        for i in range(3):
            ps = rot.next()
            proj(ps, i, hbt)
            if b > 0:
                OP("dve", "tensor_copy", cb[i].v(cb[i].h[:, 0:3]), cb[i].v(cb[i].h[:, 512:515]))
            OP("act", "activation", cb[i].v(cb[i].h[:, 3:515]), ps[:, :], AF.Copy)
            OP("dve", "tensor_scalar", cs[i][:, :], cb[i].v(cb[i].h[:, 3:515]),
               cw.v(cw.h[:, i * 4 + 3:i * 4 + 4]), None, ALU.mult)
            for tpp in (2, 1, 0):
                OP("dve", "scalar_tensor_tensor", cs[i][:, :], cb[i].v(cb[i].h[:, tpp:tpp + 512]),
                   cw.v(cw.h[:, i * 4 + tpp:i * 4 + tpp + 1]), cs[i][:, :], ALU.mult, ALU.add)
            OP("act", "activation", cs[i][:, :], cs[i][:, :], AF.Silu)
        yield
        rs = ssq_rstd(cs[0][:, :], 1.0, 0)
        OP("dve", "scalar_tensor_tensor", gqT[:, :], cs[0][:, :], SC, rs[:, :], ALU.mult, ALU.mult)
        rs = ssq_rstd(cs[1][:, :], 1.0, 0)
        OP("dve", "tensor_tensor", gkT[:, :], cs[1][:, :], rs[:, :], ALU.mult)
        OP("act", "activation", gvb[:, :], cs[2][:, :], AF.Copy)
        ps = rot.next()
        proj(ps, 3, hbt)
        OP("act", "activation", zs[p2][:, :], ps[:, :], AF.Silu)
        yield
        psa, psb = rot.next(), rot.next()
        for kc in range(16):
            MM(psa.v(psa.h[0:1, :]), wh.v(wh.h[:, kc, 896:897]), hbt.k(kc, hbt.h[:, kc, :]),
               start=(kc == 0), stop=(kc == 15))
        for kc in range(16):
            MM(psb.v(psb.h[0:1, :]), wh.v(wh.h[:, kc, 897:898]), hbt.k(kc, hbt.h[:, kc, :]),
               start=(kc == 0), stop=(kc == 15))
        OP("act", "activation", arow[:, :], psa.v(psa.h[0:1, :]), AF.Exp, bias=hvec.v(hvec.h[0:1, 4:5]))
        OP("act", "activation", arow[:, :], arow[:, :], AF.Ln, bias=epsc.v(epsc.h[0:1, 1:2]))
        OP("dve", "tensor_scalar", arow[:, :], arow[:, :], negA.v(negA.h[0:1, 0:1]), None, ALU.mult)
        OP("act", "activation", brow[:, :], psb.v(psb.h[0:1, :]), AF.Exp, scale=-1.0)
        OP("dve", "tensor_scalar", brow[:, :], brow[:, :], 1.0, None, ALU.add)
        OP("dve", "reciprocal", brow[:, :], brow[:, :])
        for c in range(8):
            OP("dve", "tensor_tensor_scan", Grow.v(Grow.h[0:1, c * 64:(c + 1) * 64]), onesrow[:, :],
               arow.v(arow.h[0:1, c * 64:(c + 1) * 64]), 0.0, ALU.mult, ALU.add)
        psG, psB, psC = rot.next(), rot.next(), rot.next()
        MM(psG[:, :], V(ones_f.ap[0:1, :], ones_f.toks), Grow[:, :])
        MM(psB.v(psB.h[0:64, :]), V(ones_f.ap[0:1, 0:64], ones_f.toks), brow[:, :])
        for c in range(8):
            MM(psC.v(psC.h[0:64, c:c + 1]), Grow.v(Grow.h[0:1, c * 64:(c + 1) * 64]),
               V(ones_f.ap[0:1, 0:1], ones_f.toks))
            MM(psC.v(psC.h[0:64, 8 + c:9 + c]), brow.v(brow.h[0:1, c * 64:(c + 1) * 64]),
               V(ones_f.ap[0:1, 0:1], ones_f.toks))
        OP("act", "activation", Gb[:, :], psG[:, :], AF.Copy)
        OP("act", "activation", EGb[p2][:, :], psG[:, :], AF.Exp)
        OP("act", "activation", Bb[:, :], psB.v(psB.h[0:64, :]), AF.Copy)
        OP("act", "activation", GBcol[:, :], psC.v(psC.h[0:64, 0:16]), AF.Copy)
        OP("act", "activation", EGcol[:, :], GBcol.v(GBcol.h[:, 0:8]), AF.Exp)
        OP("dve", "tensor_tensor", bexp[:, :], EGcol[:, :], GBcol.v(GBcol.h[:, 8:16]), ALU.mult)
        for c in range(8):
            OP("dve", "tensor_scalar", Dm.v(Dm.h[:, c, :]), Gb.v(Gb.h[0:64, c * 64:(c + 1) * 64]),
               GBcol.v(GBcol.h[:, c:c + 1]), 0.0, ALU.subtract, ALU.min)
        OP("act", "activation", Dm[:, :, :], Dm[:, :, :], AF.Exp)
        f3 = lambda v_: V(v_.ap.rearrange("p (c i) -> p c i", c=8), v_.toks)
        OP("dve", "tensor_tensor", Eq[:, :, :], Dm[:, :, :], f3(m_incl), ALU.mult)
        OP("dve", "tensor_tensor", Ea[:, :, :], Dm[:, :, :], f3(m_strict), ALU.mult)
        OP("dve", "tensor_tensor", Ea[:, :, :], Ea[:, :, :],
           Bb.v(Bb.h[:, :].rearrange("p (c i) -> p c i", c=8)), ALU.mult)
        yield
        psK, psV = rot.next(), rot.next()
        pk = psK.h[0:64, :].bitcast(BF16).rearrange("p (c d) -> p c d", c=8)
        pv = psV.h[0:64, :].bitcast(BF16).rearrange("p (c d) -> p c d", c=8)
        for c in range(8):
            TR(psK.v(pk[:, c, :]), c64(gkT, c), ident_bf[:, :])
        for c in range(8):
            TR(psV.v(pv[:, c, :]), c64(gvb, c), ident_bf[:, :])
        bc = lambda ap: ap.unsqueeze(2).broadcast_to([64, 8, 128])
        OP("dve", "tensor_tensor", vb[:, :, :], psV.v(pv), GBcol.v(bc(GBcol.h[:, 8:16])), ALU.mult)
        OP("dve", "tensor_tensor", kbg[:, :, :], psK.v(pk), bexp.v(bc(bexp.h[:, :])), ALU.mult)
        OP("dve", "tensor_tensor", kdec[p2][:, :, :], psK.v(pk), Eq.v(bc(Eq.h[:, :, 63])), ALU.mult)
        psKK, psQK = rot.next(), rot.next()
        kk = psKK.h[0:64, :].rearrange("p (c i) -> p c i", c=8)
        qk = psQK.h[0:64, :].rearrange("p (c i) -> p c i", c=8)
        for c in range(8):
            MM(psKK.v(kk[:, c, :]), c64(gkT, c), c64(gkT, c))
        for c in range(8):
            MM(psQK.v(qk[:, c, :]), c64(gkT, c), c64(gqT, c))
        OP("dve", "tensor_tensor", Pm[0][:, :, :], psKK.v(kk), Ea[:, :, :], ALU.mult)
        OP("dve", "tensor_tensor", Aqk[p2][:, :, :], psQK.v(qk), Eq[:, :, :], ALU.mult)
        yield
        psT = rot.next()
        tq = psT.h[0:64, :].rearrange("p (c i) -> p c i", c=8)
        for c in range(8):
            TR(psT.v(tq[:, c, :]), Pm[0].v(Pm[0].h[:, c, :]), V(ident_f.ap[0:64, 0:64], ident_f.toks))
        OP("act", "activation", Qm[0][:, :, :], psT.v(tq), AF.Copy)
        OP("dve", "tensor_tensor", Xm[:, :, :], V(eye8.ap.rearrange("p (c i) -> p c i", c=8), eye8.toks),
           Pm[0][:, :, :], ALU.subtract)
        for lvl in range(1, 6):
            cur, nxt = (lvl - 1) % 2, lvl % 2
            if lvl <= 4:
                psP = rot.next()
                pp = psP.h[0:64, :].rearrange("p (c i) -> p c i", c=8)
                for c in range(8):
                    MM(psP.v(pp[:, c, :]), Qm[cur].v(Qm[cur].h[:, c, :]), Pm[cur].v(Pm[cur].h[:, c, :]))
            psQ = rot.next()
            pq = psQ.h[0:64, :].rearrange("p (c i) -> p c i", c=8)
            for c in range(8):
                MM(psQ.v(pq[:, c, :]), Pm[cur].v(Pm[cur].h[:, c, :]), Qm[cur].v(Qm[cur].h[:, c, :]))
            if lvl <= 4:
                OP("act", "activation", Pm[nxt][:, :, :], psP.v(pp), AF.Copy)
            OP("dve", "tensor_copy", Qm[nxt][:, :, :], psQ.v(pq))
            psX = rot.next()
            px = psX.h[0:64, :].rearrange("p (c i) -> p c i", c=8)
            for c in range(8):
                MM(psX.v(px[:, c, :]), Qm[nxt].v(Qm[nxt].h[:, c, :]), Xm.v(Xm.h[:, c, :]))
            OP("dve", "tensor_tensor", Xm[:, :, :], Xm[:, :, :], psX.v(px), ALU.add)
            yield
        OP("act", "activation", TTb[:, :, :], Xm[:, :, :], AF.Copy)
        for hf in range(2):
            psU = rot.next()
            pu_ = psU.h[0:64, :].rearrange("p (c e) -> p c e", c=4)
            for c in range(4):
                cc = hf * 4 + c
                MM(psU.v(pu_[:, c, :]), TTb.v(TTb.h[:, cc, :]), vb.v(vb.h[:, cc, :]))
            OP("act", "activation", u[p2].v(u[p2].h[:, hf * 4:hf * 4 + 4, :]), psU.v(pu_), AF.Copy)
        psW = rot.next()
        for c in range(8):
            MM(psW.v(psW.h[:, c * 64:(c + 1) * 64]), kbg.v(kbg.h[:, c, :]), TTb.v(TTb.h[:, c, :]))
        OP("act", "activation", wT[p2][:, :], psW[:, :], AF.Copy)
        OP("dve", "tensor_tensor", qgT[p2][:, :], gqT[:, :], EGb[p2][:, :], ALU.mult)
        yield

    def chain(b):
        p2 = b % 2
        for c in range(8):
            ps1 = rot.next()
            MM(ps1.v(ps1.h[0:64, 0:128]), c64(wT[p2], c), Sb[:, :])
            vn = vnew[c % 2]
            OP("dve", "tensor_tensor", vn[:, :], u[p2].v(u[p2].h[:, c, :]), ps1.v(ps1.h[0:64, 0:128]),
               ALU.subtract)
            MM(psO.v(psO.h[:, c * 64:(c + 1) * 64]), Sb[:, :], c64(qgT[p2], c), start=True, stop=False)
            MM(psO.v(psO.h[:, c * 64:(c + 1) * 64]), vn[:, :], Aqk[p2].v(Aqk[p2].h[:, c, :]),
               start=False, stop=True)
            ps4 = rot.next()
            MM(ps4.v(ps4.h[:, 0:128]), kdec[p2].v(kdec[p2].h[:, c, :]), vn[:, :])
            OP("dve", "scalar_tensor_tensor", Sst[:, :], Sst[:, :],
               EGb[p2].v(EGb[p2].h[:, c * 64 + 63:c * 64 + 64]), ps4.v(ps4.h[:, 0:128]), ALU.mult, ALU.add)
            OP("act", "activation", Sb[:, :], Sst[:, :], AF.Copy)
            yield
        OP("act", "activation", o32[:, :], psO[:, :], AF.Copy)
        rs = ssq_rstd(o32[:, :], 1.0 / 128, 2)
        OP("dve", "scalar_tensor_tensor", o32[:, :], o32[:, :], hvec.v(hvec.h[:, 0:1]), rs[:, :],
           ALU.mult, ALU.mult)
        OP("dve", "tensor_tensor", ogT[p2][:, :], o32[:, :], zs[p2][:, :], ALU.mult)
        S.dma("sp", ag2_in.ap()[0:128, b * 512:(b + 1) * 512], ogT[p2].h[:, :],
              reads=ogT[p2][:, :].toks, writes=[g["newtok"](tk_ag2_in)])
        yield

    def attn(b):
        p2 = b % 2
        hbt = hb[p2]
        base = 4 * p2
        ps = rot.next()
        proj(ps, 4, hbt)
        OP("act", "activation", at0[:, :], ps[:, :], AF.Copy)
        rs = ssq_rstd(at0[:, :], 1.0 / 128, 1)
        OP("dve", "scalar_tensor_tensor", aqT[:, :], at0[:, :], hvec.v(hvec.h[:, 1:2]), rs[:, :],
           ALU.mult, ALU.mult)
        ps = rot.next()
        proj(ps, 5, hbt)
        OP("act", "activation", at0[:, :], ps[:, :], AF.Copy)
        rs = ssq_rstd(at0[:, :], 1.0 / 128, 1)
        OP("dve", "scalar_tensor_tensor",
           kring.v(kring.h[:, base:base + 4, :].rearrange("p m k -> p (m k)")), at0[:, :],
           hvec.v(hvec.h[:, 2:3]), rs[:, :], ALU.mult, ALU.mult)
        ps = rot.next()
        proj(ps, 6, hbt)
        OP("act", "activation", avb[:, :], ps[:, :], AF.Copy)
        psVt = rot.next()
        pvt = psVt.h[:, :].bitcast(BF16)[:, 0:512].rearrange("p (m e) -> p m e", m=4)
        for m in range(4):
            TR(psVt.v(pvt[:, m, :]), avb.v(avb.h[:, m * 128:(m + 1) * 128]), ident_bf[:, :])
        OP("dve", "tensor_copy", vring.v(vring.h[:, base:base + 4, :]), psVt.v(pvt))
        yield
        for m in range(4):
            pb = 4 * b + m
            rl = [r for r in range(5) if pb - 4 + r >= 0]
            for idx, r in enumerate(rl):
                sl = (pb - 4 + r) % 8
                psS = rot.next()
                MM(psS.v(psS.h[:, 0:128]), kring.v(kring.h[:, sl, :]), aqT.v(aqT.h[:, m * 128:(m + 1) * 128]))
                tt, ee = tS[idx % 2], eT[idx % 2]
                OP("dve", "scalar_tensor_tensor", tt[:, :], psS.v(psS.h[:, 0:128]), SC,
                   bt.v(bt.h[:, r, :]), ALU.mult, ALU.add)
                OP("act", "activation", ee[:, :], tt[:, :], AF.Exp)
                MM(psAO.v(psAO.h[:, m * 128:(m + 1) * 128]), vring.v(vring.h[:, sl, :]), ee[:, :],
                   start=(idx == 0), stop=(idx == len(rl) - 1), inc=True)
                MM(psDen.v(psDen.h[:, m * 128:(m + 1) * 128]), ones_bf[:, :], ee[:, :],
                   start=(idx == 0), stop=(idx == len(rl) - 1), inc=True)
            yield
        OP("dve", "reciprocal", rden[:, :], psDen[:, :])
        OP("dve", "tensor_tensor", oaT[p2][:, :], psAO[:, :], rden[:, :], ALU.mult)
        S.dma("sp", ag2_in.ap()[128:256, b * 512:(b + 1) * 512], oaT[p2].h[:, :],
              reads=oaT[p2][:, :].toks, writes=[g["newtok"](tk_ag2_in)])
        yield

    def drain(gen):
        for _ in gen:
            pass

    def hb_load(b):
        hbt = hb[b % 2]
        src, hf = b // 2, b % 2
        for kc in range(16):
            S.dma("sp", hbt.h[:, kc, :], g1[hf][src][:, kc, :], reads=[tk_ag1_out[hf]],
                  writes=hbt.k(kc, hbt.h[:, kc, :]).toks)

    def interleave(gens, weights):
        live = list(zip(gens, weights))
        while live:
            nxt = []
            for gen, w in live:
                alive = True
                for _ in range(w):
                    try:
                        next(gen)
                    except StopIteration:
                        alive = False
                        break
                if alive:
                    nxt.append((gen, w))
            live = nxt

    print("mixer sbuf use", R0.off, R2.off)
    PIPE = os.environ.get("MK_PIPE", "2")
    if PIPE == "0":
        hb_load(0)
        hb_load(1)
        for b in range(NB):
            drain(prep(b))
            drain(attn(b))
            if b + 2 < NB:
                hb_load(b + 2)
            drain(chain(b))
    elif PIPE == "2":
        hb_load(0)
        drain(prep(0))
        drain(attn(0))
        for b in range(NB):
            gens, wts = [chain(b)], [1]
            if b + 1 < NB:
                hb_load(b + 1)
                gens += [prep(b + 1), attn(b + 1)]
                wts += [1, 1]
            interleave(gens, wts)
    else:
        hb_load(0)
        hb_load(1)
        drain(prep(0))
        drain(attn(0))
        for b in range(NB):
            if b + 2 < NB:
                hb_load(b + 2)
            gens, wts = [chain(b)], [1]
            if b + 1 < NB:
                gens += [prep(b + 1), attn(b + 1)]
                wts += [2, 1]
            interleave(gens, wts)


def _host_layout(inp):
    f = lambda a: np.ascontiguousarray(a, dtype=np.float32)
    x = inp["x"][0]
    nrm = np.stack([inp["ffn1_norm"][0], inp["mix_norm"][0], inp["ffn2_norm"][0]])
    nrm = f(nrm.reshape(3, 16, 128).transpose(2, 0, 1).reshape(128, 48))

    def gu(wg, wu):
        a = np.stack([wg.reshape(16, 128, NF, 128), wu.reshape(16, 128, NF, 128)], axis=3)
        return f(a.transpose(2, 1, 0, 3, 4).reshape(NF, 128, 4096))

    def dn(wd):
        w = np.zeros((44 * 128, D), np.float32)
        w[:5504] = wd
        w = w.reshape(2, 22, 128, 16, 128)
        return f(w.transpose(0, 3, 2, 1, 4).reshape(2, 16, 128, 22 * 128))

    shared = {
        "nrm": nrm,
        "wgu1": gu(inp["ffn1_w_gate"][0], inp["ffn1_w_up"][0]), "wd1": dn(inp["ffn1_w_down"][0]),
        "wgu2": gu(inp["ffn2_w_gate"][0], inp["ffn2_w_up"][0]), "wd2": dn(inp["ffn2_w_down"][0]),
    }
    w_in = inp["w_in"][0]
    gg = w_in[:, 7184:9232].reshape(16, 128, 16, 128)
    ga = w_in[:, 9232:11280].reshape(16, 128, 16, 128)
    shared["wgt"] = f(np.stack([gg, ga], axis=3).transpose(2, 1, 0, 3, 4).reshape(16, 128, 4096))
    bg = inp["w_branch_gdn"][0].reshape(8, 128, 16, 128)
    ba = inp["w_branch_att"][0].reshape(8, 128, 16, 128)
    shared["wbr"] = f(np.stack([bg, ba], axis=3).transpose(2, 1, 0, 3, 4).reshape(16, 128, 2048))
    shared["wo"] = f(inp["w_out"][0].reshape(16, 128, 16, 128).transpose(2, 1, 0, 3).reshape(16, 128, 2048))
    cst = np.zeros((128, 1856), np.float32)
    cst[:, 0:128] = 1.0
    cst[:, 128:256] = np.eye(128, dtype=np.float32)
    jj, ii = np.meshgrid(np.arange(64), np.arange(64), indexing="ij")
    cst[0:64, 256:320] = (jj <= ii)
    cst[0:64, 320:832] = np.tile((ii >= jj).astype(np.float32), (1, 8))
    cst[0:64, 832:1344] = np.tile((ii > jj).astype(np.float32), (1, 8))
    cst[0:64, 1344:1856] = np.tile(np.eye(64, dtype=np.float32), (1, 8))
    shared["cst"] = cst
    conv = inp["gdn_conv"][0]
    rel = inp["att_rel_bias"][0]
    maps = []
    for c in range(NCORE):
        m = dict(shared)
        xs = x[c * TC:(c + 1) * TC, :].T
        m["xT"] = f(xs.reshape(16, 128, TC).transpose(1, 0, 2).reshape(128, 16 * TC))
        cols = []
        for base in (0, 1024, 2048, 3072, 4112, 5136, 6160):
            cols.append(w_in[:, base + c * 128: base + (c + 1) * 128])
        cols.append(w_in[:, 4096 + c: 4097 + c])
        cols.append(w_in[:, 4104 + c: 4105 + c])
        whh = np.concatenate(cols, axis=1)
        m["wh"] = f(whh.reshape(16, 128, 898).transpose(1, 0, 2).reshape(128, 16 * 898))
        cwl = np.stack([conv[:, j * 1024 + c * 128: j * 1024 + (c + 1) * 128] for j in range(3)])
        m["cw"] = f(cwl.transpose(2, 0, 1).reshape(128, 12))
        hv = np.zeros((128, 8), np.float32)
        hv[:, 0] = inp["gdn_out_norm"][0]
        hv[:, 1] = inp["att_q_norm"][0]
        hv[:, 2] = inp["att_k_norm"][0]
        hv[:, 3] = inp["gdn_A_log"][0, c]
        hv[:, 4] = inp["gdn_dt_bias"][0, c]
        m["hvec"] = hv
        j = np.arange(128)[:, None]
        i = np.arange(128)[None, :]
        tiles = []
        for r in range(5):
            idx = np.clip(128 * (4 - r) + i - j, -63, 256) + 63
            tiles.append(rel[c][idx])
        m["bt"] = f(np.stack(tiles, axis=1).reshape(128, 5 * 128))
        maps.append(m)
    return maps


_NC_CACHE = {}


def kernel(**inputs):
    inp = {k: np.asarray(v) for k, v in inputs.items()}
    stop = os.environ.get("MK_STOP") or None
    if stop not in _NC_CACHE:
        _NC_CACHE[stop] = build(stop)
    nc = _NC_CACHE[stop]
    maps = _host_layout(inp)
    res = run_bass_kernel_spmd(nc, maps, core_ids=list(range(NCORE)))
    out = np.empty((1, SEQ, D), np.float32)
    for c in range(NCORE):
        y = np.asarray(res.results[c]["yT"]).reshape(128, 16, TC)
        out[0, c * TC:(c + 1) * TC, :] = y.transpose(2, 1, 0).reshape(TC, D)
    return out
```
